# Optimizing a Trainium2 kernel written in Bass

```python
import math
import jax
import jax.numpy as jnp
from jax import lax
import numpy as np

D_MODEL = 4096
BATCH = 4
SEQ = 2048
DEPTH = 1

A_HEAD_DIM = 128
A_HEADS = D_MODEL // (2 * A_HEAD_DIM)
IDX_HEADS = 32
IDX_HEAD_DIM = 128
IDX_ROPE_DIM = 64
TOPK_MAX = 256
V_HEAD_DIM = 128
B_HEADS = D_MODEL // (2 * V_HEAD_DIM)
Q_LORA_RANK = 1024
KV_LORA_RANK = 512
QK_NOPE_DIM = 128
QK_ROPE_DIM = 64
D_FF = 11008
CONV_WIDTH = 3
REL_BUCKETS = 32
REL_MAX_DIST = 128
ROPE_THETA = 10000.0
Q_BLOCK = 128
LN_EPS = 1e-5
RMS_EPS = 1e-6
NEG_INF = -1e30
DEEPNORM_ALPHA = (2 * DEPTH) ** 0.25
DEEPNORM_BETA = (8 * DEPTH) ** -0.25
MIX_WIDTH = A_HEADS * A_HEAD_DIM + B_HEADS * V_HEAD_DIM
IN_SPLITS = (A_HEADS * A_HEAD_DIM, A_HEAD_DIM, A_HEAD_DIM,
             IDX_HEADS * IDX_HEAD_DIM, IDX_HEAD_DIM, IDX_HEADS,
             Q_LORA_RANK, KV_LORA_RANK, QK_ROPE_DIM)
IN_WIDTH = sum(IN_SPLITS)
SPLIT_POINTS = [int(v) for v in np.cumsum(IN_SPLITS)[:-1]]

kernel_name = 'hybrid_dsa_mla_convffn_layer'


def layer_norm(x, g, b):
    xf = x.astype(jnp.float32)
    mu = jnp.mean(xf, axis=-1, keepdims=True)
    var = jnp.mean(jnp.square(xf - mu), axis=-1, keepdims=True)
    y = (xf - mu) * lax.rsqrt(var + LN_EPS)
    return (y * g.astype(jnp.float32) + b.astype(jnp.float32)).astype(x.dtype)


def rms_norm(x, g):
    xf = x.astype(jnp.float32)
    y = xf * lax.rsqrt(jnp.mean(jnp.square(xf), axis=-1, keepdims=True) + RMS_EPS)
    return (y * g.astype(jnp.float32)).astype(x.dtype)


def rope(x, pos):
    half = x.shape[-1] // 2
    freqs = ROPE_THETA ** (-jnp.arange(half, dtype=jnp.float32) / half)
    ang = pos.astype(jnp.float32)[:, :, None] * freqs
    cos = jnp.cos(ang)[:, :, None, :].astype(x.dtype)
    sin = jnp.sin(ang)[:, :, None, :].astype(x.dtype)
    x1, x2 = x[..., :half], x[..., half:]
    return jnp.concatenate([x1 * cos - x2 * sin, x1 * sin + x2 * cos], axis=-1)


def rel_bucket(rel):
    n = jnp.maximum(rel, 0)
    max_exact = REL_BUCKETS // 2
    nf = jnp.maximum(n, 1).astype(jnp.float32)
    large = max_exact + (jnp.log(nf / max_exact) / math.log(REL_MAX_DIST / max_exact)
                         * (REL_BUCKETS - max_exact)).astype(jnp.int32)
    large = jnp.minimum(large, REL_BUCKETS - 1)
    return jnp.where(n < max_exact, n, large)


def dsa_attention(q, k, v, iq, ik, iw, pos, rel_bias):
    B_, S = q.shape[0], q.shape[1]
    topk = min(TOPK_MAX, S // 4)
    nope = IDX_HEAD_DIM - IDX_ROPE_DIM
    iq = jnp.concatenate([iq[..., :nope], rope(iq[..., nope:], pos)], axis=-1)
    ik_h = ik[:, :, None, :]
    ik = jnp.concatenate([ik_h[..., :nope], rope(ik_h[..., nope:], pos)], axis=-1)[:, :, 0, :]
    iw = iw * (IDX_HEADS ** -0.5)
    idx_scale = IDX_HEAD_DIM ** -0.5
    att_scale = A_HEAD_DIM ** -0.5
    key_idx = jnp.arange(S)
    gather = jax.vmap(lambda arr, ii: arr[ii])

    def block(i):
        t0 = i * Q_BLOCK
        qb = lax.dynamic_slice_in_dim(q, t0, Q_BLOCK, axis=1)
        iqb = lax.dynamic_slice_in_dim(iq, t0, Q_BLOCK, axis=1)
        iwb = lax.dynamic_slice_in_dim(iw, t0, Q_BLOCK, axis=1)
        posb = lax.dynamic_slice_in_dim(pos, t0, Q_BLOCK, axis=1)
        tq = t0 + jnp.arange(Q_BLOCK)
        causal = key_idx[None, :] <= tq[:, None]
        dots = jnp.einsum('bthd,bsd->bths', iqb, ik, preferred_element_type=jnp.float32) * idx_scale
        score = jnp.einsum('bths,bth->bts', jax.nn.relu(dots), iwb.astype(jnp.float32))
        score = jnp.where(causal[None], score, NEG_INF)
        _, sel = lax.top_k(score, topk)
        valid = sel <= tq[None, :, None]
        k_sel = gather(k, sel)
        v_sel = gather(v, sel)
        pos_sel = gather(pos, sel)
        bias = rel_bias[rel_bucket(posb[:, :, None] - pos_sel)]
        logits = jnp.einsum('bthd,btkd->bhtk', qb, k_sel, preferred_element_type=jnp.float32) * att_scale
        logits = logits + jnp.transpose(bias, (0, 3, 1, 2)).astype(jnp.float32)
        logits = jnp.where(valid[:, None], logits, NEG_INF)
        p = jax.nn.softmax(logits, axis=-1).astype(v.dtype)
        return jnp.einsum('bhtk,btkd->bthd', p, v_sel)

    out = lax.map(block, jnp.arange(S // Q_BLOCK))
    return jnp.moveaxis(out, 0, 1).reshape(B_, S, A_HEADS * A_HEAD_DIM)


def mla_attention(q_lat, kv_lat, k_rope, pos, q_norm_g, w_uq, kv_norm_g, w_ukv):
    B_, S = q_lat.shape[0], q_lat.shape[1]
    q = (rms_norm(q_lat, q_norm_g) @ w_uq).reshape(B_, S, B_HEADS, QK_NOPE_DIM + QK_ROPE_DIM)
    q = jnp.concatenate([q[..., :QK_NOPE_DIM], rope(q[..., QK_NOPE_DIM:], pos)], axis=-1)
    kv = (rms_norm(kv_lat, kv_norm_g) @ w_ukv).reshape(B_, S, B_HEADS, QK_NOPE_DIM + V_HEAD_DIM)
    k_r = jnp.broadcast_to(rope(k_rope[:, :, None, :], pos), (B_, S, B_HEADS, QK_ROPE_DIM))
    k = jnp.concatenate([kv[..., :QK_NOPE_DIM], k_r], axis=-1)
    v = kv[..., QK_NOPE_DIM:]
    scale = (QK_NOPE_DIM + QK_ROPE_DIM) ** -0.5
    key_idx = jnp.arange(S)

    def block(i):
        t0 = i * Q_BLOCK
        qb = lax.dynamic_slice_in_dim(q, t0, Q_BLOCK, axis=1)
        tq = t0 + jnp.arange(Q_BLOCK)
        logits = jnp.einsum('bthd,bshd->bhts', qb, k, preferred_element_type=jnp.float32) * scale
        logits = jnp.where((key_idx[None, :] <= tq[:, None])[None, None], logits, NEG_INF)
        p = jax.nn.softmax(logits, axis=-1).astype(v.dtype)
        return jnp.einsum('bhts,bshd->bthd', p, v)

    out = lax.map(block, jnp.arange(S // Q_BLOCK))
    return jnp.moveaxis(out, 0, 1).reshape(B_, S, B_HEADS * V_HEAD_DIM)


def causal_dwconv(h, w, b):
    S = h.shape[1]
    hp = jnp.pad(h, ((0, 0), (CONV_WIDTH - 1, 0), (0, 0)))
    out = b
    for j in range(CONV_WIDTH):
        out = out + w[j] * hp[:, j:j + S]
    return out


def conv_glu_ffn(u, w_gate, w_up, conv_w, conv_b, w_down):
    g = causal_dwconv(u @ w_gate, conv_w, conv_b)
    return (jax.nn.silu(g) * (u @ w_up)) @ w_down


def setup_inputs(seed: int = 0) -> dict:
    key = jax.random.key(seed)
    ks = jax.random.split(key, 24)
    f32 = jnp.float32
    L = DEPTH

    def nrm(k, shape, scale):
        return jax.random.normal(k, shape, f32) * scale

    x = nrm(ks[0], (BATCH, SEQ, D_MODEL), 1.0)
    c = nrm(ks[1], (BATCH, D_MODEL), 1.0)
    positions = (jax.random.randint(ks[2], (BATCH, 1), 0, 1024, dtype=jnp.int32)
                 + jnp.arange(SEQ, dtype=jnp.int32)[None, :])
    w_ada = nrm(ks[3], (L, D_MODEL, 6 * D_MODEL), 0.1 * D_MODEL ** -0.5)
    b_ada = nrm(ks[4], (L, 6 * D_MODEL), 0.01)
    col_scale = np.ones((IN_WIDTH,), np.float32)
    v_off = IN_SPLITS[0] + IN_SPLITS[1]
    col_scale[v_off:v_off + A_HEAD_DIM] = DEEPNORM_BETA
    w_in = nrm(ks[5], (L, D_MODEL, IN_WIDTH), D_MODEL ** -0.5) * jnp.asarray(col_scale)
    rel_bias = nrm(ks[6], (REL_BUCKETS, A_HEADS), 0.5)
    q_norm_g = 1.0 + nrm(ks[7], (L, Q_LORA_RANK), 0.02)
    w_uq = nrm(ks[8], (L, Q_LORA_RANK, B_HEADS * (QK_NOPE_DIM + QK_ROPE_DIM)), Q_LORA_RANK ** -0.5)
    kv_norm_g = 1.0 + nrm(ks[9], (L, KV_LORA_RANK), 0.02)
    w_uk = nrm(ks[10], (L, KV_LORA_RANK, B_HEADS, QK_NOPE_DIM), KV_LORA_RANK ** -0.5)
    w_uv = nrm(ks[11], (L, KV_LORA_RANK, B_HEADS, V_HEAD_DIM), DEEPNORM_BETA * KV_LORA_RANK ** -0.5)
    w_ukv = jnp.concatenate([w_uk, w_uv], axis=-1).reshape(L, KV_LORA_RANK, B_HEADS * (QK_NOPE_DIM + V_HEAD_DIM))
    w_o = nrm(ks[12], (L, MIX_WIDTH, D_MODEL), DEEPNORM_BETA * MIX_WIDTH ** -0.5)
    ln1_g = 1.0 + nrm(ks[13], (L, D_MODEL), 0.02)
    ln1_b = nrm(ks[14], (L, D_MODEL), 0.02)
    w_gate = nrm(ks[15], (L, D_MODEL, D_FF), D_MODEL ** -0.5)
    w_up = nrm(ks[16], (L, D_MODEL, D_FF), DEEPNORM_BETA * D_MODEL ** -0.5)
    conv_w = nrm(ks[17], (L, CONV_WIDTH, D_FF), CONV_WIDTH ** -0.5)
    conv_b = nrm(ks[18], (L, D_FF), 0.02)
    w_down = nrm(ks[19], (L, D_FF, D_MODEL), DEEPNORM_BETA * D_FF ** -0.5)
    ln2_g = 1.0 + nrm(ks[20], (L, D_MODEL), 0.02)
    ln2_b = nrm(ks[21], (L, D_MODEL), 0.02)
    return {'x': x, 'c': c, 'positions': positions, 'w_ada': w_ada, 'b_ada': b_ada,
            'w_in': w_in, 'rel_bias': rel_bias, 'q_norm_g': q_norm_g, 'w_uq': w_uq,
            'kv_norm_g': kv_norm_g, 'w_ukv': w_ukv, 'w_o': w_o, 'ln1_g': ln1_g, 'ln1_b': ln1_b,
            'w_gate': w_gate, 'w_up': w_up, 'conv_w': conv_w, 'conv_b': conv_b,
            'w_down': w_down, 'ln2_g': ln2_g, 'ln2_b': ln2_b}


def reference(x, c, positions, w_ada, b_ada, w_in, rel_bias, q_norm_g, w_uq, kv_norm_g, w_ukv,
              w_o, ln1_g, ln1_b, w_gate, w_up, conv_w, conv_b, w_down, ln2_g, ln2_b):
    B_, S = x.shape[0], x.shape[1]
    c_act = jax.nn.silu(c)
    for l in range(DEPTH):
        mod = (c_act @ w_ada[l] + b_ada[l])[:, None, :]
        sh_a, sc_a, g_a, sh_f, sc_f, g_f = jnp.split(mod, 6, axis=-1)
        u = x * (1 + sc_a) + sh_a
        a_q, a_k, a_v, i_q, i_k, i_w, b_ql, b_kvl, b_kr = jnp.split(u @ w_in[l], SPLIT_POINTS, axis=-1)
        y_a = dsa_attention(a_q.reshape(B_, S, A_HEADS, A_HEAD_DIM), a_k, a_v,
                            i_q.reshape(B_, S, IDX_HEADS, IDX_HEAD_DIM), i_k, i_w,
                            positions, rel_bias)
        y_b = mla_attention(b_ql, b_kvl, b_kr, positions, q_norm_g[l], w_uq[l], kv_norm_g[l], w_ukv[l])
        mix = jnp.concatenate([y_a, y_b], axis=-1) @ w_o[l]
        x = layer_norm(DEEPNORM_ALPHA * x + (1 + g_a) * mix, ln1_g[l], ln1_b[l])
        u = x * (1 + sc_f) + sh_f
        y = conv_glu_ffn(u, w_gate[l], w_up[l], conv_w[l], conv_b[l], w_down[l])
        x = layer_norm(DEEPNORM_ALPHA * x + (1 + g_f) * y, ln2_g[l], ln2_b[l])
    return x
```

```python
import math
from contextlib import ExitStack
import numpy as np
import ml_dtypes
import concourse.bass as bass
import concourse.mybir as mybir
from concourse.bass_utils import run_bass_kernel_spmd
from concourse.alu_op_type import AluOpType as ALU

F32 = mybir.dt.float32
BF16 = mybir.dt.bfloat16
I32 = mybir.dt.int32
AF = mybir.ActivationFunctionType
AX = mybir.AxisListType

D = 4096
NCH = 32
SEQ = 2048
TW = 2048
TQ0 = 896
NQB = 9
TQ = 1152
INW = 8160
A_Q, A_K, A_V, I_Q, I_K, I_W, B_QL, B_KVL, B_KR = 0, 2048, 2176, 2304, 6400, 6528, 6560, 7584, 8096
DFF = 11008
NFF = 86
NEG = -1e30
MASKV = -1.0e9
ALPHA = 2.0 ** 0.25
ATT_SCALE = 128.0 ** -0.5
MLA_SCALE = 192.0 ** -0.5
IW_SCALE = (32.0 ** -0.5) * (128.0 ** -0.5)
TWO_PI = 2.0 * math.pi


class Buf:
    __slots__ = ("t", "w", "r", "dkey")

    def __init__(self, t):
        self.t = t
        self.w = None
        self.r = {}
        self.dkey = None


class Prog:
    def __init__(self, nc, es):
        self.nc = nc
        self.es = es
        self.engs = {"pe": nc.tensor, "act": nc.scalar, "dve": nc.vector, "pool": nc.gpsimd, "sp": nc.sync}
        self.sems = {}
        self.cnt = {}
        self.seen = {e: {} for e in self.engs}
        for e in self.engs:
            self.sems[e] = es.enter_context(nc.semaphore("s_" + e))
            self.cnt[e] = 0
        self.nd = 0
        self.scopes = [es]
        self.scope_bufs = [[]]
        self.free_dkeys = []

    def push(self):
        s = ExitStack()
        self.scopes.append(s)
        self.scope_bufs.append([])
        return s

    def barrier(self):
        for e in self.engs:
            needs = {k: v for k, v in self.cnt.items() if k != e and v > 0}
            self._wait(e, needs)

    def pop(self):
        self.barrier()
        for b in self.scope_bufs.pop():
            if b.dkey is not None:
                self.free_dkeys.append(b.dkey)
                b.dkey = None
        self.scopes.pop().close()

    def sb(self, name, shape, dt):
        self.nsb = getattr(self, "nsb", 0) + 1
        name = "%s_%d" % (name, self.nsb)
        b = Buf(self.scopes[-1].enter_context(self.nc.sbuf_tensor(name, list(shape), dt)))
        self.scope_bufs[-1].append(b)
        return b

    def ps(self, name, shape=(128, 512), dt=F32):
        return Buf(self.scopes[-1].enter_context(self.nc.psum_tensor(name, list(shape), dt)))

    def dram(self, name, shape, dt, kind=None):
        if kind is None:
            return Buf(self.nc.dram_tensor(name, list(shape), dt))
        return Buf(self.nc.dram_tensor(name, list(shape), dt, kind=kind))

    def _wait(self, e, needs):
        for key, val in needs.items():
            if self.seen[e].get(key, 0) < val:
                self.engs[e].wait_ge(self.sems[key], val)
                self.seen[e][key] = val

    def _needs(self, e, reads, writes):
        needs = {}

        def need(key, val):
            if needs.get(key, 0) < val:
                needs[key] = val

        for b in reads:
            if b.w is not None and not (e == "pe" and b.w[0] == "pe"):
                need(*b.w)
        for b in writes:
            if b.w is not None and b.w[0] != e:
                need(*b.w)
            for key, val in b.r.items():
                if key != e:
                    need(key, val)
        return needs

    def op(self, e, reads, writes, fn, sig=True):
        self._wait(e, self._needs(e, reads, writes))
        ins = fn()
        if sig:
            self.cnt[e] += 1
            ins.then_inc(self.sems[e], 1)
            val = self.cnt[e]
        else:
            val = self.cnt[e] + 1
        for b in reads:
            b.r[e] = val
        for b in writes:
            b.w = (e, val)
            b.r = {}
        return ins

    def dma(self, q, out_ap, in_ap, reads, writes, owner, **kw):
        nd = kw.pop("ndesc", 256)
        if q == "pool":
            infl = getattr(self, "pool_inflight", [])
            while infl and sum(x[2] for x in infl) + nd > 700:
                k0, v0, _ = infl.pop(0)
                self._wait("pool", {k0: v0})
            self.pool_inflight = infl
        self._wait(q, self._needs("dma", reads, writes))
        if owner.dkey is None:
            if self.free_dkeys:
                owner.dkey = self.free_dkeys.pop()
            else:
                owner.dkey = "d%d" % self.nd
                self.nd += 1
                self.sems[owner.dkey] = self.es.enter_context(self.nc.semaphore(owner.dkey))
                self.cnt[owner.dkey] = 0
        k = owner.dkey
        ins = self.engs[q].dma_start(out=out_ap, in_=in_ap, **kw)
        ins.then_inc(self.sems[k], 16)
        self.cnt[k] += 16
        val = self.cnt[k]
        if q == "pool":
            self.pool_inflight.append((k, val, nd))
        for b in reads:
            b.r[k] = val
        for b in writes:
            b.w = (k, val)
            b.r = {}
        return ins

    def finish(self, bufs):
        needs = {}
        for b in bufs:
            if b.w is not None:
                needs[b.w[0]] = max(needs.get(b.w[0], 0), b.w[1])
        self._wait("sp", needs)


class Ring:
    def __init__(self, bufs):
        self.bufs = bufs
        self.i = 0

    def next(self):
        b = self.bufs[self.i % len(self.bufs)]
        self.i += 1
        return b


def build_program(stop_after=None, dbg=()):
    nc = bass.Bass("TRN2", target_bir_lowering=False)
    es = ExitStack()
    P = Prog(nc, es)
    E = P.engs

    specs = {
        "xw": ([TW, D], F32), "c_t": ([128, NCH], F32), "posw": ([1, TW], I32), "kvalid": ([1, TW], F32),
        "halo": ([128, 1], F32), "w_ada": ([D, 6 * D], F32), "b_ada_t": ([128, 192], F32), "w_in": ([D, INW], F32),
        "rel_bias": ([32, 16], F32), "qng_t": ([128, 8], F32), "w_uq": ([1024, 3072], F32), "kvg_t": ([128, 4], F32),
        "w_ukv": ([512, 4096], F32), "w_o": ([D, D], F32), "ln1g_t": ([128, NCH], F32), "ln1b_t": ([128, NCH], F32),
        "w_gate": ([D, DFF], F32), "w_up": ([D, DFF], F32), "convw_t": ([128, 3 * NFF], F32), "convb_t": ([128, NFF], F32),
        "w_down": ([DFF, D], F32), "ln2g_t": ([128, NCH], F32), "ln2b_t": ([128, NCH], F32),
        "c_ident": ([128, 128], F32), "c_rot": ([128, 128], F32), "c_freq": ([128, 1], F32),
        "c_wmask": ([128, 32 * 128], BF16), "c_tri": ([128, 128], F32), "c_zb": ([32, 384], F32),
    }
    used = {}

    class _In:
        def __getattr__(self, name):
            if name not in used:
                shp, dt = specs[name]
                used[name] = Buf(nc.dram_tensor(name, list(shp), dt, kind="ExternalInput"))
            return used[name]

    IN = _In()
    outs = {}

    def eout(name, shape, dt=F32):
        b = Buf(nc.dram_tensor(name, list(shape), dt, kind="ExternalOutput"))
        outs[name] = b
        return b

    out_d = eout("out", [1024, D])

    qTa_d = P.dram("qTa_d", [16, 128, TQ], BF16)
    iqT_d = P.dram("iqT_d", [32, 128, TQ], BF16)
    qTbn_d = P.dram("qTbn_d", [16, 128, TQ], BF16)
    qTbr_d = P.dram("qTbr_d", [16, 128, TQ], BF16)
    yT_d = P.dram("yT_d", [32, 128, TQ], BF16)
    x1_d = P.dram("x1_d", [TQ, D], F32)
    u2T_d = P.dram("u2T_d", [NCH, 128, TQ], BF16)
    hT_d = P.dram("hT_d", [NFF, 128, 1024], BF16)

    ident = P.sb("ident", [128, 128], F32)
    identb = P.sb("identb", [128, 128], BF16)
    rot = P.sb("rot", [128, 128], F32)
    modT = P.sb("modT", [128, 192], F32)
    ones_f = P.sb("ones_f", [128, 128], F32)
    P.dma("sp", ident.t[:], IN.c_ident.t.ap(), [IN.c_ident], [ident], ident)
    P.dma("sp", rot.t[:], IN.c_rot.t.ap(), [IN.c_rot], [rot], rot)
    P.op("dve", [ident], [identb], lambda: E["dve"].tensor_copy(out=identb.t[:], in_=ident.t[:]))
    P.op("dve", [], [ones_f], lambda: E["dve"].memset(ones_f.t[:], 1.0))

    ps_ab = ExitStack()
    psr = Ring([Buf(ps_ab.enter_context(nc.psum_tensor("psr%d" % i, [128, 512], F32))) for i in range(6)])
    psx = [Buf(ps_ab.enter_context(nc.psum_tensor("psx%d" % i, [128, 512], F32))) for i in range(2)]

    def dump(name, buf, ap, shape, dt=F32):
        if name in dbg:
            o = eout("dbg_" + name, shape, dt)
            P.dma("sp", o.t.ap(), ap, [buf], [o], buf)

    cact = P.sb("cact", [128, NCH], BF16)
    badaT = P.sb("badaT", [128, 192], F32)
    P.push()
    cin = P.sb("cin", [128, NCH], F32)
    P.dma("sp", cin.t[:], IN.c_t.t.ap(), [IN.c_t], [cin], cin)
    P.dma("sp", badaT.t[:], IN.b_ada_t.t.ap(), [IN.b_ada_t], [badaT], badaT)
    P.op("act", [cin], [cact], lambda: E["act"].activation(out=cact.t[:], in_=cin.t[:], func=AF.Silu))
    wring = Ring([P.sb("wada%d" % i, [128, NCH, 512], BF16) for i in range(3)])
    psm = psx[0]
    w_ada_v = IN.w_ada.t.ap().rearrange("(kc p) n -> p kc n", p=128)
    for g in range(16):
        wb = wring.next()
        P.dma("pool", wb.t[:], w_ada_v[:, :, g * 512:(g + 1) * 512], [IN.w_ada], [wb], wb)
        for j in range(4):
            col = g * 4 + j
            for kc in range(NCH):
                P.op("pe", [wb, cact], [psm],
                     lambda kc=kc, j=j, col=col, wb=wb: E["pe"].matmul(
                         psm.t[:, col:col + 1], wb.t[:, kc, j * 128:(j + 1) * 128], cact.t[:, kc:kc + 1],
                         start=(kc == 0), stop=(kc == NCH - 1)),
                     sig=(kc == NCH - 1))
    P.op("dve", [psm, badaT], [modT],
         lambda: E["dve"].tensor_tensor(out=modT.t[:, 0:64], in0=psm.t[:, 0:64], in1=badaT.t[:, 0:64], op=ALU.add))
    P.op("dve", [modT], [modT],
         lambda: E["dve"].tensor_scalar(out=modT.t[:, 32:64], in0=modT.t[:, 32:64], scalar1=1.0, scalar2=None, op0=ALU.add))
    P.pop()
    a2_state = {"next": 64}

    def emit_A2(k, ring, psring):
        for _ in range(k):
            col = a2_state["next"]
            if col >= 192:
                return
            a2_state["next"] = col + 1
            wb = ring.next()
            P.dma("pool", wb.t[:], w_ada_v[:, :, col * 128:(col + 1) * 128], [IN.w_ada], [wb], wb)
            pm = psring.next()
            for kc in range(NCH):
                P.op("pe", [wb, cact], [pm], lambda kc=kc, wb=wb, pm=pm: E["pe"].matmul(
                    pm.t[:, 0:1], wb.t[:, kc, :], cact.t[:, kc:kc + 1], start=(kc == 0), stop=(kc == NCH - 1)), sig=(kc == NCH - 1))
            P.op("act", [pm, badaT], [modT], lambda col=col, pm=pm: E["act"].activation(
                out=modT.t[:, col:col + 1], in_=pm.t[:, 0:1], func=AF.Identity, bias=badaT.t[:, col:col + 1]))

    dump("modT", modT, modT.t[:], [128, 192])
    SH_A, SC_A, G_A, SH_F, SC_F, G_F = 0, 32, 64, 96, 128, 160

    if stop_after == "A":
        P.finish(list(outs.values()))
        return nc, es, used, outs

    P.push()
    kTa = P.sb("kTa", [128, TW], BF16)
    Va = P.sb("Va", [128, 16, 128], BF16)
    ikT = P.sb("ikT", [128, TW], BF16)
    kvnT = P.sb("kvnT", [128, 4, TW], BF16)
    krT = P.sb("krT", [128, TW], BF16)
    P.op("dve", [], [krT], lambda: E["dve"].memset(krT.t[0:64, :], 0.0))
    wT2 = P.sb("wT2", [128, 288], F32)
    rq = P.sb("rq", [128, NQB], F32)

    P.push()
    cosF = P.sb("cosF", [128, TW], F32)
    sinF = P.sb("sinF", [128, TW], F32)
    P.push()
    posb = P.sb("posb", [128, TW], F32)
    ang = P.sb("ang", [128, TW], F32)
    kf = P.sb("kf", [128, TW], F32)
    ki = P.sb("ki", [128, TW], I32)
    freq = P.sb("freq", [128, 1], F32)
    P.dma("sp", freq.t[:], IN.c_freq.t.ap(), [IN.c_freq], [freq], freq)
    P.dma("pool", posb.t[:], IN.posw.t.ap().partition_broadcast(128), [IN.posw], [posb], posb)
    for tab, shift in ((sinF, 0.0), (cosF, math.pi / 2)):
        P.op("dve", [posb, freq], [ang],
             lambda shift=shift: E["dve"].tensor_scalar(out=ang.t[:], in0=posb.t[:], scalar1=freq.t[:, 0:1], scalar2=shift,
                                                        op0=ALU.mult, op1=ALU.add))
        P.op("dve", [ang], [kf], lambda: E["dve"].tensor_scalar(out=kf.t[:], in0=ang.t[:], scalar1=1.0 / TWO_PI, scalar2=None, op0=ALU.mult))
        P.op("dve", [kf], [ki], lambda: E["dve"].tensor_copy(out=ki.t[:], in_=kf.t[:]))
        P.op("dve", [ki], [kf], lambda: E["dve"].tensor_copy(out=kf.t[:], in_=ki.t[:]))
        C1 = 6.28125
        C2 = TWO_PI - C1
        P.op("dve", [kf, ang], [ang], lambda: E["dve"].scalar_tensor_tensor(out=ang.t[:], in0=kf.t[:], scalar=-C1, in1=ang.t[:], op0=ALU.mult, op1=ALU.add))
        P.op("dve", [kf, ang], [ang], lambda: E["dve"].scalar_tensor_tensor(out=ang.t[:], in0=kf.t[:], scalar=-C2, in1=ang.t[:], op0=ALU.mult, op1=ALU.add))
        P.op("dve", [ang], [kf], lambda: E["dve"].tensor_scalar(out=kf.t[:], in0=ang.t[:], scalar1=math.pi, scalar2=-TWO_PI, op0=ALU.is_gt, op1=ALU.mult))
        P.op("dve", [kf, ang], [ang], lambda: E["dve"].tensor_tensor(out=ang.t[:], in0=ang.t[:], in1=kf.t[:], op=ALU.add))
        P.op("dve", [ang], [kf], lambda: E["dve"].tensor_scalar(out=kf.t[:], in0=ang.t[:], scalar1=-math.pi, scalar2=TWO_PI, op0=ALU.is_lt, op1=ALU.mult))
        P.op("dve", [kf, ang], [ang], lambda: E["dve"].tensor_tensor(out=ang.t[:], in0=ang.t[:], in1=kf.t[:], op=ALU.add))
        P.op("dve", [ang], [ang], lambda: E["dve"].tensor_scalar(out=ang.t[:], in0=ang.t[:], scalar1=3.1415925, scalar2=-3.1415925, op0=ALU.min, op1=ALU.max))
        P.op("act", [ang], [tab], lambda tab=tab: E["act"].activation(out=tab.t[:], in_=ang.t[:], func=AF.Sin))
    P.op("dve", [], [cosF], lambda: E["dve"].memset(cosF.t[0:64, :], 1.0))
    P.op("dve", [], [sinF], lambda: E["dve"].memset(sinF.t[0:64, :], 0.0))
    P.pop()
    dump("cosF", cosF, cosF.t[:], [128, TW])
    dump("sinF", sinF, sinF.t[:], [128, TW])

    u_d = P.dram("u_d", [NCH, 128, TQ], BF16)
    xT_d = P.dram("xT_d", [NCH, 128, TQ], F32)

    def rope_evac(ps, n, t0, dst_buf, dst_ap, rows=slice(0, 128), g_ap=None):
        x32 = x32r.next()
        P.op("act", [ps], [x32], lambda: E["act"].activation(out=x32.t[:, :n], in_=ps.t[:, :n], func=AF.Copy))
        pr = psr.next()
        P.op("pe", [x32, rot], [pr], lambda: E["pe"].matmul(pr.t[:, :n], rot.t[:], x32.t[:, :n], start=True, stop=True))
        t1 = x32r.next()
        P.op("dve", [x32, cosF], [t1], lambda: E["dve"].tensor_tensor(out=t1.t[rows, :n], in0=x32.t[rows, :n], in1=cosF.t[rows, t0:t0 + n], op=ALU.mult))
        t2 = x32r.next()
        P.op("dve", [pr, sinF], [t2], lambda: E["dve"].tensor_tensor(out=t2.t[rows, :n], in0=pr.t[rows, :n], in1=sinF.t[rows, t0:t0 + n], op=ALU.mult))
        P.op("dve", [t1, t2], [dst_buf], lambda: E["dve"].tensor_tensor(out=dst_ap, in0=t1.t[rows, :n], in1=t2.t[rows, :n], op=ALU.add))

    P.push()
    x32r = Ring([P.sb("x32_%d" % i, [128, 512], F32) for i in range(6)])
    wK = P.sb("wK", [128, NCH, 960], BF16)
    w_in_v = IN.w_in.t.ap().rearrange("(kc p) n -> p kc n", p=128)
    P.dma("pool", wK.t[:, :, 0:256], w_in_v[:, :, A_K:A_K + 256], [IN.w_in], [wK], wK)
    P.dma("pool", wK.t[:, :, 256:384], w_in_v[:, :, I_K:I_K + 128], [IN.w_in], [wK], wK)
    P.dma("pool", wK.t[:, :, 384:960], w_in_v[:, :, B_KVL:B_KVL + 576], [IN.w_in], [wK], wK)
    kvg = P.sb("kvg", [128, 4], F32)
    P.dma("sp", kvg.t[:], IN.kvg_t.t.ap(), [IN.kvg_t], [kvg], kvg)
    xring = Ring([P.sb("xblk%d" % i, [128, D], F32) for i in range(1)])
    xTs = P.sb("xTs", [128, NCH, 128], F32)
    uring = Ring([P.sb("uT%d" % i, [128, NCH, 512], BF16) for i in range(1)])
    sqr = Ring([P.sb("sq%d" % i, [128, 512], F32) for i in range(2)])
    rrow = P.sb("rrow", [1, 512], F32)
    bc_sb = P.sb("bc_sb", [128, 512], F32)
    kvps = [P.ps("kvps%d" % i) for i in range(0)]
    xw_v = IN.xw.t.ap()
    for tt in range(4):
        uT = uring.next()
        for bb in range(4):
            blk = tt * 4 + bb
            xb = xring.next()
            P.dma("sp", xb.t[:], xw_v[blk * 128:(blk + 1) * 128, :], [IN.xw], [xb], xb)
            for c4 in range(8):
                pt = psr.next()
                for cc in range(4):
                    c = c4 * 4 + cc
                    P.op("pe", [xb, ident], [pt],
                         lambda c=c, cc=cc, pt=pt, xb=xb: E["pe"].transpose(pt.t[:, cc * 128:(cc + 1) * 128], xb.t[:, c * 128:(c + 1) * 128], ident.t[:]),
                         sig=(cc == 3))
                if blk >= 7:
                    P.op("act", [pt], [xTs], lambda c4=c4, pt=pt: E["act"].activation(
                        out=xTs.t[:, c4 * 4:(c4 + 1) * 4, :], in_=pt.t[:].rearrange("p (c t) -> p c t", c=4), func=AF.Copy))
                for cc in range(4):
                    c = c4 * 4 + cc
                    eng = "act" if (cc % 2 == 0) else "dve"
                    dsts = [(uT, uT.t[:, c, bb * 128:(bb + 1) * 128])]
                    for (db, dap) in dsts:
                        if eng == "act":
                            P.op("act", [pt, modT], [db],
                                 lambda c=c, cc=cc, pt=pt, dap=dap: E["act"].activation(
                                     out=dap, in_=pt.t[:, cc * 128:(cc + 1) * 128], func=AF.Identity,
                                     scale=modT.t[:, SC_A + c:SC_A + c + 1], bias=modT.t[:, SH_A + c:SH_A + c + 1]))
                        else:
                            P.op("dve", [pt, modT], [db],
                                 lambda c=c, cc=cc, pt=pt, dap=dap: E["dve"].tensor_scalar(
                                     out=dap, in0=pt.t[:, cc * 128:(cc + 1) * 128],
                                     scalar1=modT.t[:, SC_A + c:SC_A + c + 1], scalar2=modT.t[:, SH_A + c:SH_A + c + 1],
                                     op0=ALU.mult, op1=ALU.add))
            if blk >= 7:
                P.dma("sp", xT_d.t.ap().rearrange("c p t -> p c t")[:, :, (blk - 7) * 128:(blk - 6) * 128],
                      xTs.t[:], [xTs], [xT_d], xTs)
                P.dma("sp", u_d.t.ap().rearrange("c p t -> p c t")[:, :, (blk - 7) * 128:(blk - 6) * 128],
                      uT.t[:, :, bb * 128:(bb + 1) * 128], [uT], [u_d], uT)
        t0 = tt * 512

        def kproj(colb, pt_):
            for kc in range(NCH):
                P.op("pe", [wK, uT], [pt_],
                     lambda kc=kc: E["pe"].matmul(pt_.t[:, :], wK.t[:, kc, colb * 128:(colb + 1) * 128], uT.t[:, kc, :],
                                                  start=(kc == 0), stop=(kc == NCH - 1)), sig=(kc == NCH - 1))
        pt = psr.next()
        kproj(0, pt)
        P.op("act", [pt], [kTa], lambda pt=pt: E["act"].activation(out=kTa.t[:, t0:t0 + 512], in_=pt.t[:], func=AF.Copy))
        for bb in range(4):
            pv = psr.next()
            for kc in range(NCH):
                P.op("pe", [wK, uT], [pv],
                     lambda kc=kc, pv=pv, bb=bb: E["pe"].matmul(pv.t[:, 0:128], uT.t[:, kc, bb * 128:(bb + 1) * 128], wK.t[:, kc, 128:256],
                                                                start=(kc == 0), stop=(kc == NCH - 1)), sig=(kc == NCH - 1))
            P.op("act", [pv], [Va], lambda pv=pv, bb=bb: E["act"].activation(out=Va.t[:, tt * 4 + bb, :], in_=pv.t[:, 0:128], func=AF.Copy))
        pt = psr.next()
        kproj(2, pt)
        rope_evac(pt, 512, t0, ikT, ikT.t[:, t0:t0 + 512])
        pt = psr.next()
        for kc in range(NCH):
            P.op("pe", [wK, uT], [pt],
                 lambda kc=kc, pt=pt: E["pe"].matmul(pt.t[:, :], wK.t[:, kc, 832:960], uT.t[:, kc, :], start=(kc == 0), stop=(kc == NCH - 1)),
                 sig=(kc == NCH - 1))
        rope_evac(pt, 512, t0, krT, krT.t[64:128, t0:t0 + 512], rows=slice(64, 128))
        kvp = []
        pss = psx[1]
        for ch in range(4):
            pk = psr.next()
            kproj(3 + ch, pk)
            kvp.append(pk)
            sq = sqr.next()
            P.op("act", [pk], [sq], lambda pk=pk, sq=sq: E["act"].activation(out=sq.t[:], in_=pk.t[:], func=AF.Square))
            P.op("pe", [sq, ones_f], [pss],
                 lambda sq=sq, ch=ch: E["pe"].matmul(pss.t[0:1, :], ones_f.t[:, 0:1], sq.t[:], start=(ch == 0), stop=(ch == 3)), sig=(ch == 3))
        P.op("act", [pss], [rrow], lambda: E["act"].activation(out=rrow.t[:], in_=pss.t[0:1, :], func=AF.Sqrt, scale=1.0 / 512.0, bias=1e-6))
        P.op("dve", [rrow], [rrow], lambda: E["dve"].reciprocal(out=rrow.t[:], in_=rrow.t[:]))
        pb = psx[0]
        P.op("pe", [rrow, ones_f], [pb], lambda: E["pe"].matmul(pb.t[:, :], ones_f.t[0:1, :], rrow.t[:], start=True, stop=True))
        P.op("act", [pb], [bc_sb], lambda: E["act"].activation(out=bc_sb.t[:], in_=pb.t[:], func=AF.Copy))
        for ch in range(4):
            P.op("dve", [kvp[ch], bc_sb, kvg], [kvnT],
                 lambda ch=ch: E["dve"].scalar_tensor_tensor(out=kvnT.t[:, ch, t0:t0 + 512], in0=kvp[ch].t[:], scalar=kvg.t[:, ch:ch + 1],
                                                             in1=bc_sb.t[:], op0=ALU.mult, op1=ALU.mult))
    P.pop()
    dump("kTa", kTa, kTa.t[:], [128, TW], BF16)
    dump("ikT", ikT, ikT.t[:], [128, TW], BF16)
    dump("krT", krT, krT.t[:], [128, TW], BF16)
    dump("kvnT", kvnT, kvnT.t[:], [128, 4, TW], BF16)
    dump("Va", Va, Va.t[:], [128, 16, 128], BF16)

    if stop_after == "B1":
        P.finish(list(outs.values()))
        return nc, es, used, outs

    P.push()
    x32r = Ring([P.sb("x32b_%d" % i, [128, 512], F32) for i in range(6)])
    uTq = P.sb("uTq", [128, NCH, TQ], BF16)
    P.dma("sp", uTq.t[:], u_d.t.ap().rearrange("c p t -> p c t"), [u_d], [uTq], uTq)
    qlT = P.sb("qlT", [128, 8, TQ], BF16)
    wqr = Ring([P.sb("wq%d" % i, [128, NCH, 256], BF16) for i in range(2)])
    stg = Ring([P.sb("stg%d" % i, [128, 512], BF16) for i in range(4)])
    QT = [(0, 512), (512, 512), (1024, 128)]
    w_in_v = IN.w_in.t.ap().rearrange("(kc p) n -> p kc n", p=128)

    def gemm_q(col0, ncols, evac):
        for g in range(ncols // 256):
            wb = wqr.next()
            P.dma("pool", wb.t[:], w_in_v[:, :, col0 + g * 256:col0 + (g + 1) * 256], [IN.w_in], [wb], wb)
            for cb in range(2):
                for (t0, n) in QT:
                    pt = psr.next()
                    for kc in range(NCH):
                        P.op("pe", [wb, uTq], [pt],
                             lambda kc=kc, pt=pt, wb=wb, cb=cb, t0=t0, n=n: E["pe"].matmul(
                                 pt.t[:, :n], wb.t[:, kc, cb * 128:(cb + 1) * 128], uTq.t[:, kc, t0:t0 + n],
                                 start=(kc == 0), stop=(kc == NCH - 1)), sig=(kc == NCH - 1))
                    evac(g * 2 + cb, pt, t0, n)

    def ev_aq(h, pt, t0, n):
        st = stg.next()
        P.op("act", [pt], [st], lambda: E["act"].activation(out=st.t[:, :n], in_=pt.t[:, :n], func=AF.Copy))
        P.dma("sp", qTa_d.t.ap()[h, :, t0:t0 + n], st.t[:, :n], [st], [qTa_d], st)

    def ev_iq(h, pt, t0, n):
        st = stg.next()
        rope_evac(pt, n, TQ0 + t0, st, st.t[:, :n])
        P.dma("sp", iqT_d.t.ap()[h, :, t0:t0 + n], st.t[:, :n], [st], [iqT_d], st)

    def ev_ql(ch, pt, t0, n):
        P.op("act", [pt], [qlT], lambda: E["act"].activation(out=qlT.t[:, ch, t0:t0 + n], in_=pt.t[:, :n], func=AF.Copy))

    gemm_q(A_Q, 2048, ev_aq)
    if stop_after == "B2a":
        P.barrier(); P.finish(list(outs.values())); return nc, es, used, outs
    gemm_q(I_Q, 4096, ev_iq)
    gemm_q(B_QL, 1024, ev_ql)
    P.push()
    wiw = P.sb("wiw", [128, NCH, 32], BF16)
    P.dma("pool", wiw.t[:], w_in_v[:, :, I_W:I_W + 32], [IN.w_in], [wiw], wiw)
    iw_d = P.dram("iw_d", [TQ, 32], F32)
    iwt = P.sb("iwt", [128, 32], F32)
    for j in range(NQB):
        pt = psr.next()
        for kc in range(NCH):
            P.op("pe", [wiw, uTq], [pt], lambda kc=kc, pt=pt, j=j: E["pe"].matmul(
                pt.t[:, 0:32], uTq.t[:, kc, j * 128:(j + 1) * 128], wiw.t[:, kc, :], start=(kc == 0), stop=(kc == NCH - 1)),
                sig=(kc == NCH - 1))
        P.op("act", [pt], [iwt], lambda pt=pt: E["act"].activation(out=iwt.t[:], in_=pt.t[:, 0:32], func=AF.Copy, scale=IW_SCALE))
        P.dma("sp", iw_d.t.ap()[j * 128:(j + 1) * 128, :], iwt.t[:], [iwt], [iw_d], iwt)
    if stop_after == "B2b":
        P.barrier(); P.finish(list(outs.values())); return nc, es, used, outs
    iwg = P.sb("iwg", [32, NQB, 4, 32], F32)
    P.dma("sp", iwg.t[:], iw_d.t.ap().rearrange("(j i g) h -> g j i h", j=NQB, i=4, g=32), [iw_d], [iwg], iwg)
    iwp = P.sb("iwp", [32, NQB, 128], F32)
    P.op("dve", [iwg], [iwp], lambda: E["dve"].tensor_copy(out=iwp.t[:].rearrange("g j (h i) -> g j i h", i=4), in_=iwg.t[:]))
    if stop_after == "B2b0":
        P.barrier(); P.finish(list(outs.values())); return nc, es, used, outs
    pt = psr.next()
    for j in range(NQB):
        P.op("pe", [iwp, ident], [pt], lambda j=j, pt=pt: E["pe"].transpose(pt.t[:, j * 32:(j + 1) * 32], iwp.t[:, j, :], ident.t[0:32, 0:32]))
    P.op("dve", [pt], [wT2], lambda pt=pt: E["dve"].tensor_copy(out=wT2.t[:], in_=pt.t[:, 0:288]))
    P.pop()
    if stop_after == "B2b1":
        P.barrier(); P.finish(list(outs.values())); return nc, es, used, outs
    qng = P.sb("qng", [128, 8], F32)
    P.dma("sp", qng.t[:], IN.qng_t.t.ap(), [IN.qng_t], [qng], qng)
    rrow = P.sb("rrowq", [1, 512], F32)
    rq_d = P.dram("rq_d", [NQB, 128], F32)
    sq = P.sb("sqq", [128, 512], F32)
    pss = psx[1]
    prq = psx[0]
    for (t0, n) in QT:
        for ch in range(8):
            P.op("act", [qlT], [sq], lambda ch=ch, t0=t0, n=n: E["act"].activation(out=sq.t[:, :n], in_=qlT.t[:, ch, t0:t0 + n], func=AF.Square))
            P.op("pe", [sq, ones_f], [pss], lambda ch=ch, n=n: E["pe"].matmul(pss.t[0:1, :n], ones_f.t[:, 0:1], sq.t[:, :n], start=(ch == 0), stop=(ch == 7)))
        P.op("act", [pss], [rrow], lambda n=n: E["act"].activation(out=rrow.t[:, :n], in_=pss.t[0:1, :n], func=AF.Sqrt, scale=1.0 / 1024.0, bias=1e-6))
        P.op("dve", [rrow], [rrow], lambda n=n: E["dve"].reciprocal(out=rrow.t[:, :n], in_=rrow.t[:, :n]))
        for bb in range(n // 128):
            j = t0 // 128 + bb
            P.dma("sp", rq_d.t.ap()[j:j + 1, :], rrow.t[0:1, bb * 128:(bb + 1) * 128], [rrow], [rq_d], rrow)
    if stop_after == "B2c1":
        P.barrier(); P.finish(list(outs.values())); return nc, es, used, outs
    rqg = P.sb("rqg", [NQB, 128], F32)
    P.dma("sp", rqg.t[:], rq_d.t.ap(), [rq_d], [rqg], rqg)
    P.op("pe", [rqg, ident], [prq], lambda: E["pe"].transpose(prq.t[:, 0:NQB], rqg.t[:], ident.t[0:NQB, 0:NQB]))
    P.op("dve", [prq], [rq], lambda: E["dve"].tensor_scalar(out=rq.t[:], in0=prq.t[:, 0:NQB], scalar1=MLA_SCALE, scalar2=None, op0=ALU.mult))
    if stop_after == "B2c2":
        P.barrier(); P.finish(list(outs.values())); return nc, es, used, outs
    for ch in range(8):
        P.op("dve", [qlT, qng], [qlT], lambda ch=ch: E["dve"].tensor_scalar(out=qlT.t[:, ch, :], in0=qlT.t[:, ch, :], scalar1=qng.t[:, ch:ch + 1], scalar2=None, op0=ALU.mult))
    if stop_after == "B2c":
        P.barrier(); P.finish(list(outs.values())); return nc, es, used, outs
    wuq_v = IN.w_uq.t.ap().rearrange("(kc p) n -> p kc n", p=128)
    wur = Ring([P.sb("wuq%d" % i, [128, 8, 192], BF16) for i in range(2)])
    for h in range(16):
        wb = wur.next()
        P.dma("pool", wb.t[:], wuq_v[:, :, h * 192:(h + 1) * 192], [IN.w_uq], [wb], wb)
        for part in range(2):
            for (t0, n) in QT:
                pt = psr.next()
                for kc in range(8):
                    P.op("pe", [wb, qlT], [pt], lambda kc=kc, pt=pt, wb=wb, part=part, t0=t0, n=n: E["pe"].matmul(
                        pt.t[:, :n], wb.t[:, kc, part * 64:part * 64 + 128], qlT.t[:, kc, t0:t0 + n], start=(kc == 0), stop=(kc == 7)), sig=(kc == 7))
                st = stg.next()
                if part == 0:
                    P.op("act", [pt], [st], lambda pt=pt, st=st, n=n: E["act"].activation(out=st.t[:, :n], in_=pt.t[:, :n], func=AF.Copy))
                    P.dma("sp", qTbn_d.t.ap()[h, :, t0:t0 + n], st.t[:, :n], [st], [qTbn_d], st)
                else:
                    rope_evac(pt, n, TQ0 + t0, st, st.t[64:128, :n], rows=slice(64, 128))
                    P.dma("sp", qTbr_d.t.ap()[h, 64:128, t0:t0 + n], st.t[64:128, :n], [st], [qTbr_d], st)
    P.pop()
    P.pop()
    ps_ab.close()
    dump("wT2", wT2, wT2.t[:], [128, 288])
    dump("rq", rq, rq.t[:], [128, NQB])
    if stop_after == "B2":
        for nm, bd, shp in (("qTa_d", qTa_d, [16, 128, TQ]), ("iqT_d", iqT_d, [32, 128, TQ]), ("qTbn_d", qTbn_d, [16, 128, TQ]), ("qTbr_d", qTbr_d, [16, 128, TQ])):
            if nm in dbg:
                o = eout("dbg_" + nm, shp, BF16)
                tmp = P.sb("tmp_" + nm, [128, shp[0], TQ], BF16)
                P.dma("sp", tmp.t[:], bd.t.ap().rearrange("h p t -> p h t"), [bd], [tmp], tmp)
                P.dma("sp", o.t.ap().rearrange("h p t -> p h t"), tmp.t[:], [tmp], [o], tmp)
                P.barrier()
        P.finish(list(outs.values()))
        return nc, es, used, outs
    P.push()
    pbig = P.ps("pbig", (128, 2048))
    psc = Ring([P.ps("psc%d" % i) for i in range(4)])
    maskK = P.sb("maskK", [128, TW], F32)
    P.dma("sp", maskK.t[:], IN.kvalid.t.ap().partition_broadcast(128), [IN.kvalid], [maskK], maskK)
    tri = P.sb("tri", [128, 128], F32)
    P.dma("sp", tri.t[:], IN.c_tri.t.ap(), [IN.c_tri], [tri], tri)
    wmask = P.sb("wmask", [128, 32, 128], BF16)
    P.dma("sp", wmask.t[:], IN.c_wmask.t.ap().rearrange("p (g t) -> p g t", g=32), [IN.c_wmask], [wmask], wmask)
    NBs = P.sb("NBs", [128, 16, 256], F32)
    dump("wmask", wmask, wmask.t[:], [128, 32, 128], BF16)
    P.push()
    relb = P.sb("relb", [32, 16], F32)
    relb31 = P.sb("relb31", [32, 16], F32)
    zb = P.sb("zb", [32, 384], F32)
    P.dma("sp", relb.t[:], IN.rel_bias.t.ap(), [IN.rel_bias], [relb], relb)
    P.dma("sp", relb31.t[:], IN.rel_bias.t.ap()[31:32, :].partition_broadcast(32), [IN.rel_bias], [relb31], relb31)
    P.dma("sp", zb.t[:], IN.c_zb.t.ap(), [IN.c_zb], [zb], zb)
    P.op("dve", [relb, relb31], [relb], lambda: E["dve"].tensor_tensor(out=relb.t[:], in0=relb.t[:], in1=relb31.t[:], op=ALU.subtract))
    P.op("dve", [relb], [relb], lambda: E["dve"].tensor_scalar(out=relb.t[:], in0=relb.t[:], scalar1=1.0 / ATT_SCALE, scalar2=None, op0=ALU.mult))
    for half in range(2):
        for sl in range(128):
            sp_ = half * 128 + sl
            P.op("pe", [zb, relb], [pbig], lambda sl=sl, sp_=sp_: E["pe"].matmul(
                pbig.t[:, sl * 16:(sl + 1) * 16], zb.t[:, 255 - sp_:255 - sp_ + 128], relb.t[:, :], start=True, stop=True))
        P.op("act", [pbig], [NBs], lambda half=half: E["act"].activation(
            out=NBs.t[:, :, half * 128:(half + 1) * 128], in_=pbig.t[:, :].rearrange("p (s h) -> p h s", h=16), func=AF.Copy))
    P.pop()

    def sm_exp(kend, lg, mxb, scale_ap_or_f):
        Pb, nmx, rs = Pbr.next(), nmxr.next(), rsr.next()
        if isinstance(scale_ap_or_f, float):
            P.op("dve", [mxb], [nmx], lambda: E["dve"].tensor_scalar(out=nmx.t[:], in0=mxb.t[:], scalar1=-scale_ap_or_f, scalar2=None, op0=ALU.mult))
            P.op("act", [lg, nmx], [Pb, rs], lambda: E["act"].activation(out=Pb.t[:, :kend], in_=lg.t[:, :kend], func=AF.Exp,
                                                                         scale=scale_ap_or_f, bias=nmx.t[:, 0:1], accum_out=rs.t[:, 0:1]))
        else:
            sbuf, sap = scale_ap_or_f
            P.op("dve", [mxb, sbuf], [nmx], lambda: E["dve"].tensor_scalar(out=nmx.t[:], in0=mxb.t[:], scalar1=sap, scalar2=-1.0, op0=ALU.mult, op1=ALU.mult))
            P.op("act", [lg, nmx, sbuf], [Pb, rs], lambda: E["act"].activation(out=Pb.t[:, :kend], in_=lg.t[:, :kend], func=AF.Exp,
                                                                               scale=sap, bias=nmx.t[:, 0:1], accum_out=rs.t[:, 0:1]))
        return Pb, rs

    def pv_stage(kend, Pb, rs, V_of, ydst_buf, ydst_ap):
        nkb = kend // 128
        PT, rinv, yb = PTr.next(), rinvr.next(), ybr.next()
        for k4 in range((nkb + 3) // 4):
            nb4 = min(4, nkb - k4 * 4)
            ptp = psc.next()
            ptv = ptp.t[:].bitcast(BF16)
            for q4 in range(nb4):
                kb = k4 * 4 + q4
                P.op("pe", [Pb, identb], [ptp], lambda kb=kb, q4=q4, ptv=ptv: E["pe"].transpose(
                    ptv[:, q4 * 128:(q4 + 1) * 128], Pb.t[:, kb * 128:(kb + 1) * 128], identb.t[:]))
            P.op("dve", [ptp], [PT], lambda k4=k4, nb4=nb4, ptv=ptv: E["dve"].tensor_copy(
                out=PT.t[:, k4 * 4:k4 * 4 + nb4, :].rearrange("p k q -> p (k q)"), in_=ptv[:, 0:nb4 * 128]))
        po = psc.next()
        for kb in range(nkb):
            vbuf, vap = V_of(kb)
            P.op("pe", [PT, vbuf], [po], lambda kb=kb, vap=vap: E["pe"].matmul(po.t[:, 0:128], PT.t[:, kb, :], vap, start=(kb == 0), stop=(kb == nkb - 1)))
        P.op("dve", [rs], [rinv], lambda: E["dve"].reciprocal(out=rinv.t[:], in_=rs.t[:]))
        P.op("act", [po, rinv], [yb], lambda: E["act"].activation(out=yb.t[:], in_=po.t[:, 0:128], func=AF.Identity, scale=rinv.t[:, 0:1]))
        pty = psc.next()
        ptyv = pty.t[:].bitcast(BF16)
        P.op("pe", [yb, identb], [pty], lambda: E["pe"].transpose(ptyv[:, 0:128], yb.t[:], identb.t[:]))
        P.op("act", [pty], [ydst_buf], lambda: E["act"].activation(out=ydst_ap, in_=ptyv[:, 0:128], func=AF.Copy))

    def copy_max(kend):
        lg = lgr.next()
        mxb = mxr.next()
        P.op("dve", [pbig], [lg, mxb], lambda: E["dve"].tensor_scalar(
            out=lg.t[:, :kend], in0=pbig.t[:, :kend], scalar1=1.0, scalar2=None, op0=ALU.mult, op1=ALU.max, accum_out=mxb.t[:, 0:1]))
        return lg, mxb

    sc = P.sb("sc", [128, TW], F32)
    selbb = P.sb("selbb", [128, TW], BF16)
    lgr = Ring([P.sb("lg%d" % i, [128, TW], F32) for i in range(3)])
    Pbr = Ring([P.sb("Pb%d" % i, [128, TW], BF16) for i in range(3)])
    PTr = Ring([P.sb("PT%d" % i, [128, 16, 128], BF16) for i in range(2)])
    mxr = Ring([P.sb("mxb%d" % i, [128, 1], F32) for i in range(4)])
    nmxr = Ring([P.sb("nmx%d" % i, [128, 1], F32) for i in range(4)])
    rsr = Ring([P.sb("rs%d" % i, [128, 1], F32) for i in range(4)])
    rinvr = Ring([P.sb("rinv%d" % i, [128, 1], F32) for i in range(4)])
    ybr = Ring([P.sb("yb%d" % i, [128, 128], BF16) for i in range(2)])
    ystage = P.sb("ystage", [128, 16, 128], BF16)

    P.push()
    wk = [P.sb("wk%d" % i, [128, TW], F32) for i in range(2)]
    m8 = P.sb("m8", [128, 8], F32)
    thr = P.sb("thr", [128, 1], F32)
    iqr = Ring([P.sb("iqb%d" % i, [128, 32, 128], BF16) for i in range(2)])
    Wblk = P.sb("Wblk", [128, 32, 128], BF16)
    Rr = Ring([P.sb("Rr%d" % i, [128, 512], BF16) for i in range(4)])
    qar = Ring([P.sb("qab%d" % i, [128, 16, 128], BF16) for i in range(2)])
    wa2r = Ring([P.sb("wa2_%d" % i, [128, NCH, 128], BF16) for i in range(2)])
    nblocks = NQB if stop_after not in ("C1", "C2") else (2 if stop_after == "C1" else 0)
    for j in range(nblocks):
        T0 = TQ0 + 128 * j
        kend = T0 + 128
        nkt = (kend + 511) // 512
        iqb = iqr.next()
        P.dma("sp", iqb.t[:], iqT_d.t.ap().rearrange("h p t -> p h t")[:, :, j * 128:(j + 1) * 128], [iqT_d], [iqb], iqb)
        qab = qar.next()
        P.dma("sp", qab.t[:], qTa_d.t.ap().rearrange("h p t -> p h t")[:, :, j * 128:(j + 1) * 128], [qTa_d], [qab], qab)
        for g in range(32):
            P.op("pool", [wmask, wT2], [Wblk], lambda g=g: E["pool"].tensor_scalar(
                out=Wblk.t[:, g, :], in0=wmask.t[:, g, :], scalar1=wT2.t[:, 32 * j + g:32 * j + g + 1], scalar2=1.0, op0=ALU.mult, op1=ALU.mult))
        for kt in range(nkt):
            n = min(512, kend - kt * 512)
            Rs = {}

            def dots(g, n=n, kt=kt, iqb=iqb):
                pd = psc.next()
                P.op("pe", [iqb, ikT], [pd], lambda: E["pe"].matmul(
                    pd.t[:, :n], iqb.t[:].rearrange("p h t -> p (h t)")[:, g:4096:32], ikT.t[:, kt * 512:kt * 512 + n], start=True, stop=True))
                R = Rr.next()
                if g % 2 == 0:
                    P.op("act", [pd], [R], lambda: E["act"].activation(out=R.t[:, :n], in_=pd.t[:, :n], func=AF.Relu))
                else:
                    P.op("dve", [pd], [R], lambda: E["dve"].tensor_scalar(out=R.t[:, :n], in0=pd.t[:, :n], scalar1=0.0, scalar2=None, op0=ALU.max))
                Rs[g] = R

            dots(0)
            dots(1)
            for g in range(32):
                if g + 2 < 32:
                    dots(g + 2)
                R = Rs.pop(g)
                P.op("pe", [Wblk, R], [pbig], lambda g=g, R=R, n=n, kt=kt: E["pe"].matmul(
                    pbig.t[:, kt * 512:kt * 512 + n], Wblk.t[:, g, :], R.t[:, :n], start=(g == 0), stop=(g == 31)))
            P.op("dve", [pbig, maskK], [sc], lambda kt=kt, n=n: E["dve"].tensor_tensor(
                out=sc.t[:, kt * 512:kt * 512 + n], in0=pbig.t[:, kt * 512:kt * 512 + n], in1=maskK.t[:, kt * 512:kt * 512 + n], op=ALU.add))
        P.op("dve", [sc, tri], [sc], lambda kend=kend: E["dve"].tensor_tensor(out=sc.t[:, kend - 128:kend], in0=sc.t[:, kend - 128:kend], in1=tri.t[:], op=ALU.add))
        if stop_after == "T_idx":
            continue
        emit_A2(15, wa2r, psc)
        src = sc
        for r in range(32):
            P.op("dve", [src], [m8], lambda src=src, kend=kend: E["dve"].max(out=m8.t[:], in_=src.t[:, :kend]))
            if r < 31:
                dst = wk[r % 2]
                P.op("dve", [src, m8], [dst], lambda src=src, dst=dst, kend=kend: E["dve"].match_replace(
                    out=dst.t[:, :kend], in_to_replace=m8.t[:], in_values=src.t[:, :kend], imm_value=-3.0e38))
                src = dst
        P.op("dve", [m8], [thr], lambda: E["dve"].tensor_scalar(out=thr.t[:], in0=m8.t[:, 7:8], scalar1=-1.0e29, scalar2=None, op0=ALU.max))
        P.op("dve", [sc, thr], [selbb], lambda kend=kend: E["dve"].tensor_scalar(
            out=selbb.t[:, :kend], in0=sc.t[:, :kend], scalar1=thr.t[:, 0:1], scalar2=MASKV, op0=ALU.is_lt, op1=ALU.mult))
        if stop_after == "T_topk":
            continue

        def dsa_A(h, kend=kend, nkt=nkt, qab=qab):
            for kt in range(nkt):
                n = min(512, kend - kt * 512)
                P.op("pe", [qab, kTa], [pbig], lambda kt=kt, n=n: E["pe"].matmul(
                    pbig.t[:, kt * 512:kt * 512 + n], qab.t[:, h, :], kTa.t[:, kt * 512:kt * 512 + n], start=True, stop=False))
            for kt in range(nkt):
                n = min(512, kend - kt * 512)
                P.op("pe", [identb, selbb], [pbig], lambda kt=kt, n=n: E["pe"].matmul(
                    pbig.t[:, kt * 512:kt * 512 + n], identb.t[:], selbb.t[:, kt * 512:kt * 512 + n], start=False, stop=False))
            for nb in range(2):
                c0 = kend - 256 + 128 * nb
                P.op("pe", [ident, NBs], [pbig], lambda nb=nb, c0=c0: E["pe"].matmul(
                    pbig.t[:, c0:c0 + 128], ident.t[:], NBs.t[:, h, nb * 128:(nb + 1) * 128], start=False, stop=True))
            lg, mxb = copy_max(kend)
            return sm_exp(kend, lg, mxb, ATT_SCALE)

        pend = [dsa_A(0), dsa_A(1)]
        for h in range(16):
            if h + 2 < 16:
                pend.append(dsa_A(h + 2))
            cur = pend.pop(0)
            pv_stage(kend, cur[0], cur[1], lambda kb: (Va, Va.t[:, kb, :]), ystage, ystage.t[:, h, :])
        P.dma("sp", yT_d.t.ap().rearrange("h p t -> p h t")[:, 0:16, j * 128:(j + 1) * 128], ystage.t[:], [ystage], [yT_d], ystage)
    P.pop()
    P.push()
    wa2r2 = Ring([P.sb("wa2b_%d" % i, [128, NCH, 128], BF16) for i in range(2)])
    emit_A2(200, wa2r2, psc)
    P.pop()
    for jj in (2, 4, 5):
        P.op("dve", [modT], [modT], lambda jj=jj: E["dve"].tensor_scalar(out=modT.t[:, jj * 32:(jj + 1) * 32], in0=modT.t[:, jj * 32:(jj + 1) * 32],
                                                                          scalar1=1.0, scalar2=None, op0=ALU.add))
    if stop_after in ("T_idx", "T_topk", "T_dsa"):
        P.barrier(); P.finish(list(outs.values())); return nc, es, used, outs
    P.push()
    maskM = P.sb("maskM", [128, TW], BF16)
    triM = P.sb("triM", [128, 128], BF16)
    P.op("dve", [maskK], [maskM], lambda: E["dve"].tensor_scalar(out=maskM.t[:], in0=maskK.t[:], scalar1=MASKV / NEG, scalar2=None, op0=ALU.mult))
    P.op("dve", [tri], [triM], lambda: E["dve"].tensor_scalar(out=triM.t[:], in0=tri.t[:], scalar1=MASKV / NEG, scalar2=None, op0=ALU.mult))
    wkvr = Ring([P.sb("wkv%d" % i, [128, 4, 256], BF16) for i in range(2)])
    kThr = Ring([P.sb("kTh%d" % i, [128, TW], BF16) for i in range(2)])
    Vhr = Ring([P.sb("Vh%d" % i, [128, 16, 128], BF16) for i in range(2)])
    qbnr = Ring([P.sb("qbn%d" % i, [128, TQ], BF16) for i in range(2)])
    qbrr = Ring([P.sb("qbr%d" % i, [128, TQ], BF16) for i in range(2)])
    for b_ in qbrr.bufs:
        P.op("dve", [], [b_], lambda b_=b_: E["dve"].memset(b_.t[0:64, :], 0.0))
    ymla = Ring([P.sb("ymla%d" % i, [128, TQ], BF16) for i in range(2)])
    wukv_v = IN.w_ukv.t.ap().rearrange("(kc p) n -> p kc n", p=128)
    nheads = 16 if stop_after != "C2" else 2
    hstate = {}

    def mla_setup(h):
        wkv = wkvr.next()
        P.dma("pool", wkv.t[:], wukv_v[:, :, h * 256:(h + 1) * 256], [IN.w_ukv], [wkv], wkv, ndesc=32)
        qbn, qbr, kTh, Vh, ym = qbnr.next(), qbrr.next(), kThr.next(), Vhr.next(), ymla.next()
        P.dma("sp", qbn.t[:], qTbn_d.t.ap()[h], [qTbn_d], [qbn], qbn)
        P.dma("sp", qbr.t[64:128, :], qTbr_d.t.ap()[h, 64:128, :], [qTbr_d], [qbr], qbr)
        for tt in range(4):
            pt = psc.next()
            for kc in range(4):
                P.op("pe", [wkv, kvnT], [pt], lambda kc=kc, tt=tt, pt=pt: E["pe"].matmul(
                    pt.t[:, :], wkv.t[:, kc, 0:128], kvnT.t[:, kc, tt * 512:(tt + 1) * 512], start=(kc == 0), stop=(kc == 3)))
            P.op("act", [pt], [kTh], lambda tt=tt, pt=pt: E["act"].activation(out=kTh.t[:, tt * 512:(tt + 1) * 512], in_=pt.t[:], func=AF.Copy))
        for b4 in range(4):
            pv = psc.next()
            for q4 in range(4):
                blk = b4 * 4 + q4
                for kc in range(4):
                    P.op("pe", [wkv, kvnT], [pv], lambda kc=kc, q4=q4, blk=blk, pv=pv: E["pe"].matmul(
                        pv.t[:, q4 * 128:(q4 + 1) * 128], kvnT.t[:, kc, blk * 128:(blk + 1) * 128], wkv.t[:, kc, 128:256], start=(kc == 0), stop=(kc == 3)))
            P.op("dve", [pv], [Vh], lambda b4=b4, pv=pv: E["dve"].tensor_copy(out=Vh.t[:, b4 * 4:(b4 + 1) * 4, :].rearrange("p k d -> p (k d)"), in_=pv.t[:, :]))
        hstate[h] = (qbn, qbr, kTh, Vh, ym)

    def mla_A(h, j):
        if j == 0:
            mla_setup(h)
        qbn, qbr, kTh, Vh, ym = hstate[h]
        kend = TQ0 + 128 * j + 128
        nkt = (kend + 511) // 512
        tiles = [(kt, min(512, kend - kt * 512)) for kt in range(nkt)]
        for kt, n in tiles:
            P.op("pe", [qbn, kTh], [pbig], lambda kt=kt, n=n: E["pe"].matmul(
                pbig.t[:, kt * 512:kt * 512 + n], qbn.t[:, j * 128:(j + 1) * 128], kTh.t[:, kt * 512:kt * 512 + n], start=True, stop=False))
        for kt, n in tiles:
            P.op("pe", [qbr, krT], [pbig], lambda kt=kt, n=n: E["pe"].matmul(
                pbig.t[:, kt * 512:kt * 512 + n], qbr.t[:, j * 128:(j + 1) * 128], krT.t[:, kt * 512:kt * 512 + n], start=False, stop=False))
        for kt, n in tiles:
            P.op("pe", [identb, maskM], [pbig], lambda kt=kt, n=n: E["pe"].matmul(
                pbig.t[:, kt * 512:kt * 512 + n], identb.t[:], maskM.t[:, kt * 512:kt * 512 + n], start=False, stop=False))
        P.op("pe", [identb, triM], [pbig], lambda: E["pe"].matmul(pbig.t[:, kend - 128:kend], identb.t[:], triM.t[:], start=False, stop=True))
        lg, mxb = copy_max(kend)
        Pb, rs = sm_exp(kend, lg, mxb, (rq, rq.t[:, j:j + 1]))
        return (kend, Pb, rs, Vh, ym)

    items = [(h, j) for h in range(nheads) for j in range(NQB)]
    pend = [mla_A(*it) for it in items[:2]]
    for i, (h, j) in enumerate(items):
        if i + 2 < len(items):
            pend.append(mla_A(*items[i + 2]))
        kend_, Pb_, rs_, Vh_, ym_ = pend.pop(0)
        pv_stage(kend_, Pb_, rs_, lambda kb, Vh_=Vh_: (Vh_, Vh_.t[:, kb, :]), ym_, ym_.t[:, j * 128:(j + 1) * 128])
        if j == NQB - 1:
            P.dma("sp", yT_d.t.ap()[16 + h], ym_.t[:], [ym_], [yT_d], ym_)
    P.pop()
    P.pop()
    if stop_after == "C2":
        o = eout("dbg_yT", [32, 128, TQ], BF16)
        tmp = P.sb("tmp_y", [128, 32, TQ], BF16)
        P.dma("sp", tmp.t[:], yT_d.t.ap().rearrange("h p t -> p h t"), [yT_d], [tmp], tmp)
        P.dma("sp", o.t.ap().rearrange("h p t -> p h t"), tmp.t[:], [tmp], [o], tmp)
        P.barrier()
        P.finish(list(outs.values()))
        return nc, es, used, outs
    P.pop()
    ps_de = ExitStack()
    psr = Ring([Buf(ps_de.enter_context(nc.psum_tensor("psd%d" % i, [128, 512], F32))) for i in range(6)])
    psx = [Buf(ps_de.enter_context(nc.psum_tensor("psy%d" % i, [128, 512], F32))) for i in range(2)]
    rT_d = P.dram("rT_d", [NCH, 128, TQ], F32)
    x1T_d = P.dram("x1T_d", [NCH, 128, TQ], F32)
    r2T_d = P.dram("r2T_d", [NCH, 128, 1024], F32)

    def ln_pass(src_d, ntok, tile_n, g_in, b_in, consume, tile_begin=None, tile_end=None):
        P.push()
        gT = P.sb("ln_g", [128, NCH], F32)
        bT = P.sb("ln_b", [128, NCH], F32)
        P.dma("sp", gT.t[:], g_in.t.ap(), [g_in], [gT], gT)
        P.dma("sp", bT.t[:], b_in.t.ap(), [b_in], [bT], bT)
        rall = P.sb("ln_r", [128, NCH, tile_n], F32)
        sqr_ = Ring([P.sb("ln_sq%d" % i, [128, tile_n], F32) for i in range(2)])
        rows = {k: P.sb("ln_row_" + k, [1, tile_n], F32) for k in ("m", "msq", "v", "r", "nmr")}
        bc1 = P.sb("ln_bc1", [128, tile_n], F32)
        bc2 = P.sb("ln_bc2", [128, tile_n], F32)
        xo = Ring([P.sb("ln_xo%d" % i, [128, tile_n], F32) for i in range(2)])
        ps1, ps2 = psx[0], psx[1]
        for t0 in range(0, ntok, tile_n):
            n = min(tile_n, ntok - t0)
            if tile_begin is not None:
                tile_begin(t0, n)
            P.dma("sp", rall.t[:, :, :n], src_d.t.ap().rearrange("c p t -> p c t")[:, :, t0:t0 + n], [src_d], [rall], rall)
            for c in range(NCH):
                P.op("pe", [rall, ones_f], [ps1], lambda c=c, n=n: E["pe"].matmul(ps1.t[0:1, :n], ones_f.t[:, 0:1], rall.t[:, c, :n], start=(c == 0), stop=(c == NCH - 1)))
                sq = sqr_.next()
                P.op("act", [rall], [sq], lambda c=c, n=n, sq=sq: E["act"].activation(out=sq.t[:, :n], in_=rall.t[:, c, :n], func=AF.Square))
                P.op("pe", [sq, ones_f], [ps2], lambda c=c, n=n, sq=sq: E["pe"].matmul(ps2.t[0:1, :n], ones_f.t[:, 0:1], sq.t[:, :n], start=(c == 0), stop=(c == NCH - 1)))
            m, msq, v, rr, nmr = rows["m"], rows["msq"], rows["v"], rows["r"], rows["nmr"]
            P.op("act", [ps1], [m], lambda n=n: E["act"].activation(out=m.t[:, :n], in_=ps1.t[0:1, :n], func=AF.Copy, scale=1.0 / D))
            P.op("dve", [m], [msq], lambda n=n: E["dve"].tensor_tensor(out=msq.t[:, :n], in0=m.t[:, :n], in1=m.t[:, :n], op=ALU.mult))
            P.op("dve", [ps2, msq], [v], lambda n=n: E["dve"].scalar_tensor_tensor(out=v.t[:, :n], in0=ps2.t[0:1, :n], scalar=1.0 / D, in1=msq.t[:, :n], op0=ALU.mult, op1=ALU.subtract))
            P.op("act", [v], [v], lambda n=n: E["act"].activation(out=v.t[:, :n], in_=v.t[:, :n], func=AF.Sqrt, bias=1e-5))
            P.op("dve", [v], [rr], lambda n=n: E["dve"].reciprocal(out=rr.t[:, :n], in_=v.t[:, :n]))
            P.op("dve", [m, rr], [nmr], lambda n=n: E["dve"].scalar_tensor_tensor(out=nmr.t[:, :n], in0=m.t[:, :n], scalar=-1.0, in1=rr.t[:, :n], op0=ALU.mult, op1=ALU.mult))
            P.op("pe", [rr, ones_f], [ps1], lambda n=n: E["pe"].matmul(ps1.t[:, :n], ones_f.t[0:1, :], rr.t[:, :n], start=True, stop=True))
            P.op("pe", [nmr, ones_f], [ps2], lambda n=n: E["pe"].matmul(ps2.t[:, :n], ones_f.t[0:1, :], nmr.t[:, :n], start=True, stop=True))
            for c in range(NCH):
                P.op("dve", [rall, ps1], [rall], lambda c=c, n=n: E["dve"].tensor_tensor(out=rall.t[:, c, :n], in0=rall.t[:, c, :n], in1=ps1.t[:, :n], op=ALU.mult))
                P.op("dve", [rall, ps2], [rall], lambda c=c, n=n: E["dve"].tensor_tensor(out=rall.t[:, c, :n], in0=rall.t[:, c, :n], in1=ps2.t[:, :n], op=ALU.add))
                x = xo.next()
                P.op("act", [rall, gT, bT], [x], lambda c=c, n=n, x=x: E["act"].activation(
                    out=x.t[:, :n], in_=rall.t[:, c, :n], func=AF.Identity, scale=gT.t[:, c:c + 1], bias=bT.t[:, c:c + 1]))
                consume(c, t0, n, x)
            if tile_end is not None:
                tile_end(t0, n)
        P.pop()

    P.push()
    yT = P.sb("yT", [128, 32, TQ], BF16)
    P.dma("sp", yT.t[:], yT_d.t.ap().rearrange("h p t -> p h t"), [yT_d], [yT], yT)
    wor = Ring([P.sb("wo%d" % i, [128, NCH, 256], BF16) for i in range(2)])
    xtr = Ring([P.sb("xt%d" % i, [128, 512], F32) for i in range(3)])
    rtr = Ring([P.sb("rt%d" % i, [128, 512], F32) for i in range(3)])
    QT = [(0, 512), (512, 512), (1024, 128)]
    wo_v = IN.w_o.t.ap().rearrange("(kc p) n -> p kc n", p=128)
    for cg in range(16):
        wb = wor.next()
        P.dma("pool", wb.t[:], wo_v[:, :, cg * 256:(cg + 1) * 256], [IN.w_o], [wb], wb)
        for cb in range(2):
            dc = cg * 2 + cb
            for (t0, n) in QT:
                pt = psr.next()
                for kc in range(NCH):
                    P.op("pe", [wb, yT], [pt], lambda kc=kc, pt=pt, wb=wb, cb=cb, t0=t0, n=n: E["pe"].matmul(
                        pt.t[:, :n], wb.t[:, kc, cb * 128:(cb + 1) * 128], yT.t[:, kc, t0:t0 + n], start=(kc == 0), stop=(kc == NCH - 1)), sig=(kc == NCH - 1))
                xt = xtr.next()
                P.dma("sp", xt.t[:, :n], xT_d.t.ap()[dc, :, t0:t0 + n], [xT_d], [xt], xt)
                P.op("act", [xt], [xt], lambda xt=xt, n=n: E["act"].activation(out=xt.t[:, :n], in_=xt.t[:, :n], func=AF.Copy, scale=ALPHA))
                rt = rtr.next()
                P.op("dve", [pt, xt, modT], [rt], lambda pt=pt, xt=xt, rt=rt, n=n, dc=dc: E["dve"].scalar_tensor_tensor(
                    out=rt.t[:, :n], in0=pt.t[:, :n], scalar=modT.t[:, G_A + dc:G_A + dc + 1], in1=xt.t[:, :n], op0=ALU.mult, op1=ALU.add))
                P.dma("sp", rT_d.t.ap()[dc, :, t0:t0 + n], rt.t[:, :n], [rt], [rT_d], rt)
    P.pop()
    u2r = None

    def ln1_consume(c, t0, n, x):
        P.dma("sp", x1T_d.t.ap()[c, :, t0:t0 + n], x.t[:, :n], [x], [x1T_d], x)
        u2 = u2ring.next()
        P.op("dve", [x, modT], [u2], lambda: E["dve"].tensor_scalar(out=u2.t[:, :n], in0=x.t[:, :n], scalar1=modT.t[:, SC_F + c:SC_F + c + 1],
                                                                   scalar2=modT.t[:, SH_F + c:SH_F + c + 1], op0=ALU.mult, op1=ALU.add))
        P.dma("sp", u2T_d.t.ap()[c, :, t0:t0 + n], u2.t[:, :n], [u2], [u2T_d], u2)

    P.push()
    u2ring = Ring([P.sb("u2s%d" % i, [128, 384], BF16) for i in range(3)])
    ln_pass(rT_d, TQ, 384, IN.ln1g_t, IN.ln1b_t, ln1_consume)
    P.pop()
    if stop_after == "D":
        o = eout("dbg_x1T", [NCH, 128, TQ], F32)
        tmp = P.sb("tmp_x", [128, TQ], F32)
        for c in range(NCH):
            P.dma("sp", tmp.t[:], x1T_d.t.ap()[c], [x1T_d], [tmp], tmp)
            P.dma("sp", o.t.ap()[c], tmp.t[:], [tmp], [o], tmp)
        P.barrier()
        P.finish(list(outs.values()))
        return nc, es, used, outs
    P.push()
    u2 = P.sb("u2T", [128, NCH, 1026], BF16)
    P.dma("sp", u2.t[:], u2T_d.t.ap().rearrange("c p t -> p c t")[:, :, 126:TQ], [u2T_d], [u2], u2)
    cw = P.sb("cw", [128, 3 * NFF], F32)
    cbias = P.sb("cbias", [128, NFF], F32)
    hfl = P.sb("hfl", [128, 1], F32)
    P.dma("sp", cw.t[:], IN.convw_t.t.ap(), [IN.convw_t], [cw], cw)
    P.dma("sp", cbias.t[:], IN.convb_t.t.ap(), [IN.convb_t], [cbias], cbias)
    P.dma("sp", hfl.t[:], IN.halo.t.ap(), [IN.halo], [hfl], hfl)
    wgr = Ring([P.sb("wg%d" % i, [128, NCH, 256], BF16) for i in range(2)])
    wupr = Ring([P.sb("wu%d" % i, [128, NCH, 256], BF16) for i in range(2)])
    hgr = Ring([P.sb("hgp%d" % i, [128, 1026], F32) for i in range(2)])
    gar = Ring([P.sb("ga%d" % i, [128, 1024], F32) for i in range(2)])
    sgr = Ring([P.sb("sg%d" % i, [128, 1024], F32) for i in range(2)])
    hsr = Ring([P.sb("hst%d" % i, [128, 1024], BF16) for i in range(2)])
    wg_v = IN.w_gate.t.ap().rearrange("(kc p) n -> p kc n", p=128)
    wu_v = IN.w_up.t.ap().rearrange("(kc p) n -> p kc n", p=128)
    nfg = NFF // 2
    for fg in range(nfg):
        wg = wgr.next()
        wu = wupr.next()
        P.dma("pool", wg.t[:], wg_v[:, :, fg * 256:(fg + 1) * 256], [IN.w_gate], [wg], wg)
        P.dma("pool", wu.t[:], wu_v[:, :, fg * 256:(fg + 1) * 256], [IN.w_up], [wu], wu)
        for cb in range(2):
            f = fg * 2 + cb
            hgp = hgr.next()
            ph = psr.next()
            for kc in range(NCH):
                P.op("pe", [wg, u2], [ph], lambda kc=kc, ph=ph, wg=wg, cb=cb: E["pe"].matmul(
                    ph.t[:, 0:2], wg.t[:, kc, cb * 128:(cb + 1) * 128], u2.t[:, kc, 0:2], start=(kc == 0), stop=(kc == NCH - 1)), sig=(kc == NCH - 1))
            P.op("act", [ph, hfl], [hgp], lambda ph=ph, hgp=hgp: E["act"].activation(out=hgp.t[:, 0:2], in_=ph.t[:, 0:2], func=AF.Identity, scale=hfl.t[:, 0:1]))
            for tt in range(2):
                pg = psr.next()
                for kc in range(NCH):
                    P.op("pe", [wg, u2], [pg], lambda kc=kc, pg=pg, wg=wg, cb=cb, tt=tt: E["pe"].matmul(
                        pg.t[:, :], wg.t[:, kc, cb * 128:(cb + 1) * 128], u2.t[:, kc, 2 + tt * 512:2 + (tt + 1) * 512],
                        start=(kc == 0), stop=(kc == NCH - 1)), sig=(kc == NCH - 1))
                P.op("act", [pg], [hgp], lambda pg=pg, hgp=hgp, tt=tt: E["act"].activation(out=hgp.t[:, 2 + tt * 512:2 + (tt + 1) * 512], in_=pg.t[:], func=AF.Copy))
            ga = gar.next()
            P.op("act", [hgp, cw, cbias], [ga], lambda hgp=hgp, ga=ga, f=f: E["act"].activation(
                out=ga.t[:], in_=hgp.t[:, 2:1026], func=AF.Identity, scale=cw.t[:, 2 * NFF + f:2 * NFF + f + 1], bias=cbias.t[:, f:f + 1]))
            P.op("dve", [hgp, cw, ga], [ga], lambda hgp=hgp, ga=ga, f=f: E["dve"].scalar_tensor_tensor(
                out=ga.t[:], in0=hgp.t[:, 1:1025], scalar=cw.t[:, NFF + f:NFF + f + 1], in1=ga.t[:], op0=ALU.mult, op1=ALU.add))
            P.op("dve", [hgp, cw, ga], [ga], lambda hgp=hgp, ga=ga, f=f: E["dve"].scalar_tensor_tensor(
                out=ga.t[:], in0=hgp.t[:, 0:1024], scalar=cw.t[:, f:f + 1], in1=ga.t[:], op0=ALU.mult, op1=ALU.add))
            sg = sgr.next()
            P.op("act", [ga], [sg], lambda ga=ga, sg=sg: E["act"].activation(out=sg.t[:], in_=ga.t[:], func=AF.Silu))
            hst = hsr.next()
            for tt in range(2):
                pu = psr.next()
                for kc in range(NCH):
                    P.op("pe", [wu, u2], [pu], lambda kc=kc, pu=pu, wu=wu, cb=cb, tt=tt: E["pe"].matmul(
                        pu.t[:, :], wu.t[:, kc, cb * 128:(cb + 1) * 128], u2.t[:, kc, 2 + tt * 512:2 + (tt + 1) * 512],
                        start=(kc == 0), stop=(kc == NCH - 1)), sig=(kc == NCH - 1))
                P.op("dve", [sg, pu], [hst], lambda sg=sg, pu=pu, hst=hst, tt=tt: E["dve"].tensor_tensor(
                    out=hst.t[:, tt * 512:(tt + 1) * 512], in0=sg.t[:, tt * 512:(tt + 1) * 512], in1=pu.t[:], op=ALU.mult))
            P.dma("sp", hT_d.t.ap()[f], hst.t[:], [hst], [hT_d], hst)
    P.pop()
    P.push()
    hTt = P.sb("hTt", [128, NFF, 512], BF16)
    wdr = Ring([P.sb("wd%d" % i, [128, NFF, 256], BF16) for i in range(2)])
    xtr = Ring([P.sb("x1t%d" % i, [128, 512], F32) for i in range(3)])
    rtr = Ring([P.sb("r2t%d" % i, [128, 512], F32) for i in range(3)])
    wd_v = IN.w_down.t.ap().rearrange("(f p) n -> p f n", p=128)
    for tt in range(2):
        P.dma("sp", hTt.t[:], hT_d.t.ap().rearrange("f p t -> p f t")[:, :, tt * 512:(tt + 1) * 512], [hT_d], [hTt], hTt)
        for cg in range(16):
            wd = wdr.next()
            P.dma("pool", wd.t[:], wd_v[:, :, cg * 256:(cg + 1) * 256], [IN.w_down], [wd], wd, ndesc=688)
            for cb in range(2):
                dc = cg * 2 + cb
                pt = psr.next()
                for f in range(NFF):
                    P.op("pe", [wd, hTt], [pt], lambda f=f, pt=pt, wd=wd, cb=cb: E["pe"].matmul(
                        pt.t[:, :], wd.t[:, f, cb * 128:(cb + 1) * 128], hTt.t[:, f, :], start=(f == 0), stop=(f == NFF - 1)), sig=(f == NFF - 1))
                xt = xtr.next()
                P.dma("sp", xt.t[:], x1T_d.t.ap()[dc, :, 128 + tt * 512:128 + (tt + 1) * 512], [x1T_d], [xt], xt)
                P.op("act", [xt], [xt], lambda xt=xt: E["act"].activation(out=xt.t[:], in_=xt.t[:], func=AF.Copy, scale=ALPHA))
                rt = rtr.next()
                P.op("dve", [pt, xt, modT], [rt], lambda pt=pt, xt=xt, rt=rt, dc=dc: E["dve"].scalar_tensor_tensor(
                    out=rt.t[:], in0=pt.t[:], scalar=modT.t[:, G_F + dc:G_F + dc + 1], in1=xt.t[:], op0=ALU.mult, op1=ALU.add))
                P.dma("sp", r2T_d.t.ap()[dc, :, tt * 512:(tt + 1) * 512], rt.t[:], [rt], [r2T_d], rt)
    P.pop()
    if stop_after == "E1":
        o = eout("dbg_r2", [4, 128, 1024], F32)
        tmp = P.sb("tmp_r2", [128, 1024], F32)
        for c in range(4):
            P.dma("sp", tmp.t[:], r2T_d.t.ap()[c], [r2T_d], [tmp], tmp)
            P.dma("sp", o.t.ap()[c], tmp.t[:], [tmp], [o], tmp)
        o2 = eout("dbg_hT", [4, 128, 1024], BF16)
        tmp2 = P.sb("tmp_h", [128, 1024], BF16)
        for c in range(4):
            P.dma("sp", tmp2.t[:], hT_d.t.ap()[c], [hT_d], [tmp2], tmp2)
            P.dma("sp", o2.t.ap()[c], tmp2.t[:], [tmp2], [o2], tmp2)
        P.barrier()
        P.finish(list(outs.values()))
        return nc, es, used, outs
    P.push()
    orow = [P.sb("orow%d" % i, [128, D], F32) for i in range(2)]

    def ln2_consume(c, t0, n, x):
        for bb in range(n // 128):
            pt = psr.next()
            P.op("pe", [x, ident], [pt], lambda pt=pt, bb=bb: E["pe"].transpose(pt.t[:, 0:128], x.t[:, bb * 128:(bb + 1) * 128], ident.t[:]))
            ob = orow[bb]
            if (c + bb) % 2 == 0:
                P.op("act", [pt], [ob], lambda pt=pt, ob=ob: E["act"].activation(out=ob.t[:, c * 128:(c + 1) * 128], in_=pt.t[:, 0:128], func=AF.Copy))
            else:
                P.op("dve", [pt], [ob], lambda pt=pt, ob=ob: E["dve"].tensor_copy(out=ob.t[:, c * 128:(c + 1) * 128], in_=pt.t[:, 0:128]))

    def ln2_end(t0, n):
        for bb in range(n // 128):
            P.dma("sp", out_d.t.ap()[t0 + bb * 128:t0 + (bb + 1) * 128, :], orow[bb].t[:], [orow[bb]], [out_d], orow[bb])

    ln_pass(r2T_d, 1024, 256, IN.ln2g_t, IN.ln2b_t, ln2_consume, tile_end=ln2_end)
    P.pop()
    ps_de.close()
    P.barrier()
    P.finish(list(outs.values()))
    return nc, es, used, outs


def host_prep(inputs):
    f32 = np.float32
    x = np.asarray(inputs["x"], f32)
    c = np.asarray(inputs["c"], f32)
    pos = np.asarray(inputs["positions"], np.int32)

    def tl(v, nch):
        return np.ascontiguousarray(np.asarray(v, f32).reshape(nch, 128).T)

    shared = {
        "w_ada": np.asarray(inputs["w_ada"], f32)[0],
        "b_ada_t": tl(np.asarray(inputs["b_ada"])[0], 192),
        "w_in": np.asarray(inputs["w_in"], f32)[0],
        "rel_bias": np.asarray(inputs["rel_bias"], f32),
        "qng_t": tl(np.asarray(inputs["q_norm_g"])[0], 8),
        "w_uq": np.asarray(inputs["w_uq"], f32)[0],
        "kvg_t": tl(np.asarray(inputs["kv_norm_g"])[0], 4),
        "w_ukv": np.asarray(inputs["w_ukv"], f32)[0],
        "w_o": np.asarray(inputs["w_o"], f32)[0],
        "ln1g_t": tl(np.asarray(inputs["ln1_g"])[0], 32),
        "ln1b_t": tl(np.asarray(inputs["ln1_b"])[0], 32),
        "w_gate": np.asarray(inputs["w_gate"], f32)[0],
        "w_up": np.asarray(inputs["w_up"], f32)[0],
        "convw_t": np.ascontiguousarray(np.asarray(inputs["conv_w"], f32)[0].reshape(3, NFF, 128).transpose(2, 0, 1).reshape(128, 3 * NFF)),
        "convb_t": tl(np.asarray(inputs["conv_b"])[0], NFF),
        "w_down": np.asarray(inputs["w_down"], f32)[0],
        "ln2g_t": tl(np.asarray(inputs["ln2_g"])[0], 32),
        "ln2b_t": tl(np.asarray(inputs["ln2_b"])[0], 32),
    }
    shared["c_ident"] = np.eye(128, dtype=f32)
    rot = np.zeros((128, 128), f32)
    for m in range(64, 96):
        rot[m + 32, m] = -1.0
    for m in range(96, 128):
        rot[m - 32, m] = 1.0
    shared["c_rot"] = rot
    half = 32
    fr = (np.float32(10000.0) ** (-np.arange(half, dtype=f32) / np.float32(half))).astype(f32)
    shared["c_freq"] = np.ascontiguousarray(fr[np.arange(128) % 32].reshape(128, 1))
    wm = np.zeros((128, 32, 128), f32)
    for p in range(128):
        for g in range(32):
            wm[p, g, g + 32 * (p % 4)] = 1.0
    shared["c_wmask"] = wm.reshape(128, 32 * 128).astype(ml_dtypes.bfloat16)
    tri = np.where(np.arange(128)[None, :] <= np.arange(128)[:, None], 0.0, NEG).astype(f32)
    shared["c_tri"] = tri

    def bucket(n):
        if n < 16:
            return n
        v = 16 + int(np.float32(np.log(np.float32(n) / np.float32(16.0))) / np.float32(math.log(128 / 16)) * np.float32(16))
        return min(v, 31)

    zb = np.zeros((32, 384), f32)
    for xi in range(384):
        xv = min(max(xi - 127, 0), 128)
        zb[bucket(xv), xi] = 1.0
    shared["c_zb"] = zb

    per_core = []
    for core in range(8):
        b, hf = core // 2, core % 2
        start = hf * 1024
        xw = np.zeros((TW, D), f32)
        pw = np.zeros((1, TW), np.int32)
        kv = np.full((1, TW), NEG, f32)
        lo = start - 1024
        if lo < 0:
            xw[1024:] = x[b, 0:1024]
            pw[0, 1024:] = pos[b, 0:1024]
            kv[0, 1024:] = 0.0
        else:
            xw[:] = x[b, lo:lo + TW]
            pw[0, :] = pos[b, lo:lo + TW]
            kv[0, :] = 0.0
        m = dict(shared)
        m["xw"] = xw
        m["posw"] = pw
        m["kvalid"] = kv
        m["halo"] = np.full((128, 1), float(hf), f32)
        m["c_t"] = np.ascontiguousarray(c[b].reshape(NCH, 128).T)
        per_core.append(m)
    return per_core


def kernel(**inputs):
    nc, es, used, outs = build_program()
    per_core = [{k: m[k] for k in used} for m in host_prep(inputs)]
    with es:
        res = run_bass_kernel_spmd(nc, per_core, core_ids=list(range(8)))
    out = np.zeros((4, SEQ, D), np.float32)
    for core in range(8):
        b, hf = core // 2, core % 2
        out[b, hf * 1024:(hf + 1) * 1024] = res.results[core]["out"]
    return out
```

```python
import math
from contextlib import ExitStack
import numpy as np
import ml_dtypes
import concourse.bass as bass
import concourse.mybir as mybir
from concourse.bass_utils import run_bass_kernel_spmd
from concourse.alu_op_type import AluOpType as ALU

F32 = mybir.dt.float32
BF16 = mybir.dt.bfloat16
I32 = mybir.dt.int32
AF = mybir.ActivationFunctionType
AX = mybir.AxisListType

D = 4096
NCH = 32
SEQ = 2048
TW = 2048
TQ0 = 896
NQB = 9
TQ = 1152
INW = 8160
A_Q, A_K, A_V, I_Q, I_K, I_W, B_QL, B_KVL, B_KR = 0, 2048, 2176, 2304, 6400, 6528, 6560, 7584, 8096
DFF = 11008
NFF = 86
NEG = -1e30
MASKV = -1.0e9
ALPHA = 2.0 ** 0.25
ATT_SCALE = 128.0 ** -0.5
MLA_SCALE = 192.0 ** -0.5
IW_SCALE = (32.0 ** -0.5) * (128.0 ** -0.5)
TWO_PI = 2.0 * math.pi


class Buf:
    __slots__ = ("t", "w", "r", "dkey")

    def __init__(self, t):
        self.t = t
        self.w = None
        self.r = {}
        self.dkey = None


class Prog:
    def __init__(self, nc, es):
        self.nc = nc
        self.es = es
        self.engs = {"pe": nc.tensor, "act": nc.scalar, "dve": nc.vector, "pool": nc.gpsimd, "sp": nc.sync}
        self.sems = {}
        self.cnt = {}
        self.seen = {e: {} for e in self.engs}
        for e in self.engs:
            self.sems[e] = es.enter_context(nc.semaphore("s_" + e))
            self.cnt[e] = 0
        self.nd = 0
        self.scopes = [es]
        self.scope_bufs = [[]]
        self.free_dkeys = []

    def push(self):
        s = ExitStack()
        self.scopes.append(s)
        self.scope_bufs.append([])
        return s

    def barrier(self):
        for e in self.engs:
            needs = {k: v for k, v in self.cnt.items() if k != e and v > 0}
            self._wait(e, needs)

    def pop(self):
        self.barrier()
        for b in self.scope_bufs.pop():
            if b.dkey is not None:
                self.free_dkeys.append(b.dkey)
                b.dkey = None
        self.scopes.pop().close()

    def sb(self, name, shape, dt):
        self.nsb = getattr(self, "nsb", 0) + 1
        name = "%s_%d" % (name, self.nsb)
        b = Buf(self.scopes[-1].enter_context(self.nc.sbuf_tensor(name, list(shape), dt)))
        self.scope_bufs[-1].append(b)
        return b

    def ps(self, name, shape=(128, 512), dt=F32):
        return Buf(self.scopes[-1].enter_context(self.nc.psum_tensor(name, list(shape), dt)))

    def dram(self, name, shape, dt, kind=None):
        if kind is None:
            return Buf(self.nc.dram_tensor(name, list(shape), dt))
        return Buf(self.nc.dram_tensor(name, list(shape), dt, kind=kind))

    def _wait(self, e, needs):
        for key, val in needs.items():
            if self.seen[e].get(key, 0) < val:
                self.engs[e].wait_ge(self.sems[key], val)
                self.seen[e][key] = val

    def _needs(self, e, reads, writes):
        needs = {}

        def need(key, val):
            if needs.get(key, 0) < val:
                needs[key] = val

        for b in reads:
            if b.w is not None and not (e == "pe" and b.w[0] == "pe"):
                need(*b.w)
        for b in writes:
            if b.w is not None and b.w[0] != e:
                need(*b.w)
            for key, val in b.r.items():
                if key != e:
                    need(key, val)
        return needs

    def op(self, e, reads, writes, fn, sig=True):
        self._wait(e, self._needs(e, reads, writes))
        ins = fn()
        if sig:
            self.cnt[e] += 1
            ins.then_inc(self.sems[e], 1)
            val = self.cnt[e]
        else:
            val = self.cnt[e] + 1
        for b in reads:
            b.r[e] = val
        for b in writes:
            b.w = (e, val)
            b.r = {}
        return ins

    def dma(self, q, out_ap, in_ap, reads, writes, owner, **kw):
        nd = kw.pop("ndesc", 256)
        if q == "pool":
            infl = getattr(self, "pool_inflight", [])
            while infl and sum(x[2] for x in infl) + nd > 700:
                k0, v0, _ = infl.pop(0)
                self._wait("pool", {k0: v0})
            self.pool_inflight = infl
        self._wait(q, self._needs("dma", reads, writes))
        if owner.dkey is None:
            if self.free_dkeys:
                owner.dkey = self.free_dkeys.pop()
            else:
                owner.dkey = "d%d" % self.nd
                self.nd += 1
                self.sems[owner.dkey] = self.es.enter_context(self.nc.semaphore(owner.dkey))
                self.cnt[owner.dkey] = 0
        k = owner.dkey
        ins = self.engs[q].dma_start(out=out_ap, in_=in_ap, **kw)
        ins.then_inc(self.sems[k], 16)
        self.cnt[k] += 16
        val = self.cnt[k]
        if q == "pool":
            self.pool_inflight.append((k, val, nd))
        for b in reads:
            b.r[k] = val
        for b in writes:
            b.w = (k, val)
            b.r = {}
        return ins

    def finish(self, bufs):
        needs = {}
        for b in bufs:
            if b.w is not None:
                needs[b.w[0]] = max(needs.get(b.w[0], 0), b.w[1])
        self._wait("sp", needs)


class Ring:
    def __init__(self, bufs):
        self.bufs = bufs
        self.i = 0

    def next(self):
        b = self.bufs[self.i % len(self.bufs)]
        self.i += 1
        return b


def build_program(stop_after=None, dbg=()):
    nc = bass.Bass("TRN2", target_bir_lowering=False)
    es = ExitStack()
    P = Prog(nc, es)
    E = P.engs

    specs = {
        "xw": ([TW, D], F32), "c_t": ([128, NCH], F32), "posw": ([1, TW], I32), "kvalid": ([1, TW], F32),
        "halo": ([128, 1], F32), "w_ada": ([D, 6 * D], F32), "b_ada_t": ([128, 192], F32), "w_in": ([D, INW], F32),
        "rel_bias": ([32, 16], F32), "qng_t": ([128, 8], F32), "w_uq": ([1024, 3072], F32), "kvg_t": ([128, 4], F32),
        "w_ukv": ([512, 4096], F32), "w_o": ([D, D], F32), "ln1g_t": ([128, NCH], F32), "ln1b_t": ([128, NCH], F32),
        "w_gate": ([D, DFF], F32), "w_up": ([D, DFF], F32), "convw_t": ([128, 3 * NFF], F32), "convb_t": ([128, NFF], F32),
        "w_down": ([DFF, D], F32), "ln2g_t": ([128, NCH], F32), "ln2b_t": ([128, NCH], F32),
        "c_ident": ([128, 128], F32), "c_rot": ([128, 128], F32), "c_freq": ([128, 1], F32),
        "c_wmask": ([128, 32 * 128], BF16), "c_tri": ([128, 128], F32), "c_zb": ([32, 384], F32),
    }
    used = {}

    class _In:
        def __getattr__(self, name):
            if name not in used:
                shp, dt = specs[name]
                used[name] = Buf(nc.dram_tensor(name, list(shp), dt, kind="ExternalInput"))
            return used[name]

    IN = _In()
    outs = {}

    def eout(name, shape, dt=F32):
        b = Buf(nc.dram_tensor(name, list(shape), dt, kind="ExternalOutput"))
        outs[name] = b
        return b

    out_d = eout("out", [1024, D])

    qTa_d = P.dram("qTa_d", [16, 128, TQ], BF16)
    iqT_d = P.dram("iqT_d", [32, 128, TQ], BF16)
    qTbn_d = P.dram("qTbn_d", [16, 128, TQ], BF16)
    qTbr_d = P.dram("qTbr_d", [16, 128, TQ], BF16)
    yT_d = P.dram("yT_d", [32, 128, TQ], BF16)
    x1_d = P.dram("x1_d", [TQ, D], F32)
    u2T_d = P.dram("u2T_d", [NCH, 128, TQ], BF16)
    hT_d = P.dram("hT_d", [NFF, 128, 1024], BF16)

    ident = P.sb("ident", [128, 128], F32)
    identb = P.sb("identb", [128, 128], BF16)
    rot = P.sb("rot", [128, 128], F32)
    modT = P.sb("modT", [128, 192], F32)
    ones_f = P.sb("ones_f", [128, 128], F32)
    P.dma("sp", ident.t[:], IN.c_ident.t.ap(), [IN.c_ident], [ident], ident)
    P.dma("sp", rot.t[:], IN.c_rot.t.ap(), [IN.c_rot], [rot], rot)
    P.op("dve", [ident], [identb], lambda: E["dve"].tensor_copy(out=identb.t[:], in_=ident.t[:]))
    P.op("dve", [], [ones_f], lambda: E["dve"].memset(ones_f.t[:], 1.0))

    ps_ab = ExitStack()
    psr = Ring([Buf(ps_ab.enter_context(nc.psum_tensor("psr%d" % i, [128, 512], F32))) for i in range(6)])
    psx = [Buf(ps_ab.enter_context(nc.psum_tensor("psx%d" % i, [128, 512], F32))) for i in range(2)]

    def dump(name, buf, ap, shape, dt=F32):
        if name in dbg:
            o = eout("dbg_" + name, shape, dt)
            P.dma("sp", o.t.ap(), ap, [buf], [o], buf)

    cact = P.sb("cact", [128, NCH], BF16)
    badaT = P.sb("badaT", [128, 192], F32)
    P.push()
    cin = P.sb("cin", [128, NCH], F32)
    P.dma("sp", cin.t[:], IN.c_t.t.ap(), [IN.c_t], [cin], cin)
    P.dma("sp", badaT.t[:], IN.b_ada_t.t.ap(), [IN.b_ada_t], [badaT], badaT)
    P.op("act", [cin], [cact], lambda: E["act"].activation(out=cact.t[:], in_=cin.t[:], func=AF.Silu))
    wring = Ring([P.sb("wada%d" % i, [128, NCH, 512], BF16) for i in range(3)])
    psm = psx[0]
    w_ada_v = IN.w_ada.t.ap().rearrange("(kc p) n -> p kc n", p=128)
    for g in range(16):
        wb = wring.next()
        P.dma("pool", wb.t[:], w_ada_v[:, :, g * 512:(g + 1) * 512], [IN.w_ada], [wb], wb)
        for j in range(4):
            col = g * 4 + j
            for kc in range(NCH):
                P.op("pe", [wb, cact], [psm],
                     lambda kc=kc, j=j, col=col, wb=wb: E["pe"].matmul(
                         psm.t[:, col:col + 1], wb.t[:, kc, j * 128:(j + 1) * 128], cact.t[:, kc:kc + 1],
                         start=(kc == 0), stop=(kc == NCH - 1)),
                     sig=(kc == NCH - 1))
    P.op("dve", [psm, badaT], [modT],
         lambda: E["dve"].tensor_tensor(out=modT.t[:, 0:64], in0=psm.t[:, 0:64], in1=badaT.t[:, 0:64], op=ALU.add))
    P.op("dve", [modT], [modT],
         lambda: E["dve"].tensor_scalar(out=modT.t[:, 32:64], in0=modT.t[:, 32:64], scalar1=1.0, scalar2=None, op0=ALU.add))
    P.pop()
    a2_state = {"next": 64}

    def emit_A2(k, ring, psring):
        for _ in range(k):
            col = a2_state["next"]
            if col >= 192:
                return
            a2_state["next"] = col + 1
            wb = ring.next()
            P.dma("pool", wb.t[:], w_ada_v[:, :, col * 128:(col + 1) * 128], [IN.w_ada], [wb], wb)
            pm = psring.next()
            for kc in range(NCH):
                P.op("pe", [wb, cact], [pm], lambda kc=kc, wb=wb, pm=pm: E["pe"].matmul(
                    pm.t[:, 0:1], wb.t[:, kc, :], cact.t[:, kc:kc + 1], start=(kc == 0), stop=(kc == NCH - 1)), sig=(kc == NCH - 1))
            P.op("act", [pm, badaT], [modT], lambda col=col, pm=pm: E["act"].activation(
                out=modT.t[:, col:col + 1], in_=pm.t[:, 0:1], func=AF.Identity, bias=badaT.t[:, col:col + 1]))

    dump("modT", modT, modT.t[:], [128, 192])
    SH_A, SC_A, G_A, SH_F, SC_F, G_F = 0, 32, 64, 96, 128, 160

    if stop_after == "A":
        P.finish(list(outs.values()))
        return nc, es, used, outs

    P.push()
    kTa = P.sb("kTa", [128, TW], BF16)
    Va = P.sb("Va", [128, 16, 128], BF16)
    ikT = P.sb("ikT", [128, TW], BF16)
    kvnT = P.sb("kvnT", [128, 4, TW], BF16)
    krT = P.sb("krT", [128, TW], BF16)
    P.op("dve", [], [krT], lambda: E["dve"].memset(krT.t[0:64, :], 0.0))
    wT2 = P.sb("wT2", [128, 288], F32)
    rq = P.sb("rq", [128, NQB], F32)

    P.push()
    cosF = P.sb("cosF", [128, TW], F32)
    sinF = P.sb("sinF", [128, TW], F32)
    P.push()
    posb = P.sb("posb", [128, TW], F32)
    ang = P.sb("ang", [128, TW], F32)
    kf = P.sb("kf", [128, TW], F32)
    ki = P.sb("ki", [128, TW], I32)
    freq = P.sb("freq", [128, 1], F32)
    P.dma("sp", freq.t[:], IN.c_freq.t.ap(), [IN.c_freq], [freq], freq)
    P.dma("pool", posb.t[:], IN.posw.t.ap().partition_broadcast(128), [IN.posw], [posb], posb)
    for tab, shift in ((sinF, 0.0), (cosF, math.pi / 2)):
        P.op("dve", [posb, freq], [ang],
             lambda shift=shift: E["dve"].tensor_scalar(out=ang.t[:], in0=posb.t[:], scalar1=freq.t[:, 0:1], scalar2=shift,
                                                        op0=ALU.mult, op1=ALU.add))
        P.op("dve", [ang], [kf], lambda: E["dve"].tensor_scalar(out=kf.t[:], in0=ang.t[:], scalar1=1.0 / TWO_PI, scalar2=None, op0=ALU.mult))
        P.op("dve", [kf], [ki], lambda: E["dve"].tensor_copy(out=ki.t[:], in_=kf.t[:]))
        P.op("dve", [ki], [kf], lambda: E["dve"].tensor_copy(out=kf.t[:], in_=ki.t[:]))
        C1 = 6.28125
        C2 = TWO_PI - C1
        P.op("dve", [kf, ang], [ang], lambda: E["dve"].scalar_tensor_tensor(out=ang.t[:], in0=kf.t[:], scalar=-C1, in1=ang.t[:], op0=ALU.mult, op1=ALU.add))
        P.op("dve", [kf, ang], [ang], lambda: E["dve"].scalar_tensor_tensor(out=ang.t[:], in0=kf.t[:], scalar=-C2, in1=ang.t[:], op0=ALU.mult, op1=ALU.add))
        P.op("dve", [ang], [kf], lambda: E["dve"].tensor_scalar(out=kf.t[:], in0=ang.t[:], scalar1=math.pi, scalar2=-TWO_PI, op0=ALU.is_gt, op1=ALU.mult))
        P.op("dve", [kf, ang], [ang], lambda: E["dve"].tensor_tensor(out=ang.t[:], in0=ang.t[:], in1=kf.t[:], op=ALU.add))
        P.op("dve", [ang], [kf], lambda: E["dve"].tensor_scalar(out=kf.t[:], in0=ang.t[:], scalar1=-math.pi, scalar2=TWO_PI, op0=ALU.is_lt, op1=ALU.mult))
        P.op("dve", [kf, ang], [ang], lambda: E["dve"].tensor_tensor(out=ang.t[:], in0=ang.t[:], in1=kf.t[:], op=ALU.add))
        P.op("dve", [ang], [ang], lambda: E["dve"].tensor_scalar(out=ang.t[:], in0=ang.t[:], scalar1=3.1415925, scalar2=-3.1415925, op0=ALU.min, op1=ALU.max))
        P.op("act", [ang], [tab], lambda tab=tab: E["act"].activation(out=tab.t[:], in_=ang.t[:], func=AF.Sin))
    P.op("dve", [], [cosF], lambda: E["dve"].memset(cosF.t[0:64, :], 1.0))
    P.op("dve", [], [sinF], lambda: E["dve"].memset(sinF.t[0:64, :], 0.0))
    P.pop()
    dump("cosF", cosF, cosF.t[:], [128, TW])
    dump("sinF", sinF, sinF.t[:], [128, TW])

    u_d = P.dram("u_d", [NCH, 128, TQ], BF16)
    xT_d = P.dram("xT_d", [NCH, 128, TQ], F32)

    def rope_evac(ps, n, t0, dst_buf, dst_ap, rows=slice(0, 128), g_ap=None):
        x32 = x32r.next()
        P.op("act", [ps], [x32], lambda: E["act"].activation(out=x32.t[:, :n], in_=ps.t[:, :n], func=AF.Copy))
        pr = psr.next()
        P.op("pe", [x32, rot], [pr], lambda: E["pe"].matmul(pr.t[:, :n], rot.t[:], x32.t[:, :n], start=True, stop=True))
        t1 = x32r.next()
        P.op("dve", [x32, cosF], [t1], lambda: E["dve"].tensor_tensor(out=t1.t[rows, :n], in0=x32.t[rows, :n], in1=cosF.t[rows, t0:t0 + n], op=ALU.mult))
        t2 = x32r.next()
        P.op("dve", [pr, sinF], [t2], lambda: E["dve"].tensor_tensor(out=t2.t[rows, :n], in0=pr.t[rows, :n], in1=sinF.t[rows, t0:t0 + n], op=ALU.mult))
        P.op("dve", [t1, t2], [dst_buf], lambda: E["dve"].tensor_tensor(out=dst_ap, in0=t1.t[rows, :n], in1=t2.t[rows, :n], op=ALU.add))

    P.push()
    x32r = Ring([P.sb("x32_%d" % i, [128, 512], F32) for i in range(6)])
    wK = P.sb("wK", [128, NCH, 960], BF16)
    w_in_v = IN.w_in.t.ap().rearrange("(kc p) n -> p kc n", p=128)
    P.dma("pool", wK.t[:, :, 0:256], w_in_v[:, :, A_K:A_K + 256], [IN.w_in], [wK], wK)
    P.dma("pool", wK.t[:, :, 256:384], w_in_v[:, :, I_K:I_K + 128], [IN.w_in], [wK], wK)
    P.dma("pool", wK.t[:, :, 384:960], w_in_v[:, :, B_KVL:B_KVL + 576], [IN.w_in], [wK], wK)
    kvg = P.sb("kvg", [128, 4], F32)
    P.dma("sp", kvg.t[:], IN.kvg_t.t.ap(), [IN.kvg_t], [kvg], kvg)
    xring = Ring([P.sb("xblk%d" % i, [128, D], F32) for i in range(1)])
    xTs = P.sb("xTs", [128, NCH, 128], F32)
    uring = Ring([P.sb("uT%d" % i, [128, NCH, 512], BF16) for i in range(1)])
    sqr = Ring([P.sb("sq%d" % i, [128, 512], F32) for i in range(2)])
    rrow = P.sb("rrow", [1, 512], F32)
    bc_sb = P.sb("bc_sb", [128, 512], F32)
    kvps = [P.ps("kvps%d" % i) for i in range(0)]
    xw_v = IN.xw.t.ap()
    for tt in range(4):
        uT = uring.next()
        for bb in range(4):
            blk = tt * 4 + bb
            xb = xring.next()
            P.dma("sp", xb.t[:], xw_v[blk * 128:(blk + 1) * 128, :], [IN.xw], [xb], xb)
            for c4 in range(8):
                pt = psr.next()
                for cc in range(4):
                    c = c4 * 4 + cc
                    P.op("pe", [xb, ident], [pt],
                         lambda c=c, cc=cc, pt=pt, xb=xb: E["pe"].transpose(pt.t[:, cc * 128:(cc + 1) * 128], xb.t[:, c * 128:(c + 1) * 128], ident.t[:]),
                         sig=(cc == 3))
                if blk >= 7:
                    P.op("act", [pt], [xTs], lambda c4=c4, pt=pt: E["act"].activation(
                        out=xTs.t[:, c4 * 4:(c4 + 1) * 4, :], in_=pt.t[:].rearrange("p (c t) -> p c t", c=4), func=AF.Copy))
                for cc in range(4):
                    c = c4 * 4 + cc
                    eng = "act" if (cc % 2 == 0) else "dve"
                    dsts = [(uT, uT.t[:, c, bb * 128:(bb + 1) * 128])]
                    for (db, dap) in dsts:
                        if eng == "act":
                            P.op("act", [pt, modT], [db],
                                 lambda c=c, cc=cc, pt=pt, dap=dap: E["act"].activation(
                                     out=dap, in_=pt.t[:, cc * 128:(cc + 1) * 128], func=AF.Identity,
                                     scale=modT.t[:, SC_A + c:SC_A + c + 1], bias=modT.t[:, SH_A + c:SH_A + c + 1]))
                        else:
                            P.op("dve", [pt, modT], [db],
                                 lambda c=c, cc=cc, pt=pt, dap=dap: E["dve"].tensor_scalar(
                                     out=dap, in0=pt.t[:, cc * 128:(cc + 1) * 128],
                                     scalar1=modT.t[:, SC_A + c:SC_A + c + 1], scalar2=modT.t[:, SH_A + c:SH_A + c + 1],
                                     op0=ALU.mult, op1=ALU.add))
            if blk >= 7:
                P.dma("sp", xT_d.t.ap().rearrange("c p t -> p c t")[:, :, (blk - 7) * 128:(blk - 6) * 128],
                      xTs.t[:], [xTs], [xT_d], xTs)
                P.dma("sp", u_d.t.ap().rearrange("c p t -> p c t")[:, :, (blk - 7) * 128:(blk - 6) * 128],
                      uT.t[:, :, bb * 128:(bb + 1) * 128], [uT], [u_d], uT)
        t0 = tt * 512

        def kproj(colb, pt_):
            for kc in range(NCH):
                P.op("pe", [wK, uT], [pt_],
                     lambda kc=kc: E["pe"].matmul(pt_.t[:, :], wK.t[:, kc, colb * 128:(colb + 1) * 128], uT.t[:, kc, :],
                                                  start=(kc == 0), stop=(kc == NCH - 1)), sig=(kc == NCH - 1))
        pt = psr.next()
        kproj(0, pt)
        P.op("act", [pt], [kTa], lambda pt=pt: E["act"].activation(out=kTa.t[:, t0:t0 + 512], in_=pt.t[:], func=AF.Copy))
        for bb in range(4):
            pv = psr.next()
            for kc in range(NCH):
                P.op("pe", [wK, uT], [pv],
                     lambda kc=kc, pv=pv, bb=bb: E["pe"].matmul(pv.t[:, 0:128], uT.t[:, kc, bb * 128:(bb + 1) * 128], wK.t[:, kc, 128:256],
                                                                start=(kc == 0), stop=(kc == NCH - 1)), sig=(kc == NCH - 1))
            P.op("act", [pv], [Va], lambda pv=pv, bb=bb: E["act"].activation(out=Va.t[:, tt * 4 + bb, :], in_=pv.t[:, 0:128], func=AF.Copy))
        pt = psr.next()
        kproj(2, pt)
        rope_evac(pt, 512, t0, ikT, ikT.t[:, t0:t0 + 512])
        pt = psr.next()
        for kc in range(NCH):
            P.op("pe", [wK, uT], [pt],
                 lambda kc=kc, pt=pt: E["pe"].matmul(pt.t[:, :], wK.t[:, kc, 832:960], uT.t[:, kc, :], start=(kc == 0), stop=(kc == NCH - 1)),
                 sig=(kc == NCH - 1))
        rope_evac(pt, 512, t0, krT, krT.t[64:128, t0:t0 + 512], rows=slice(64, 128))
        kvp = []
        pss = psx[1]
        for ch in range(4):
            pk = psr.next()
            kproj(3 + ch, pk)
            kvp.append(pk)
            sq = sqr.next()
            P.op("act", [pk], [sq], lambda pk=pk, sq=sq: E["act"].activation(out=sq.t[:], in_=pk.t[:], func=AF.Square))
            P.op("pe", [sq, ones_f], [pss],
                 lambda sq=sq, ch=ch: E["pe"].matmul(pss.t[0:1, :], ones_f.t[:, 0:1], sq.t[:], start=(ch == 0), stop=(ch == 3)), sig=(ch == 3))
        P.op("act", [pss], [rrow], lambda: E["act"].activation(out=rrow.t[:], in_=pss.t[0:1, :], func=AF.Sqrt, scale=1.0 / 512.0, bias=1e-6))
        P.op("dve", [rrow], [rrow], lambda: E["dve"].reciprocal(out=rrow.t[:], in_=rrow.t[:]))
        pb = psx[0]
        P.op("pe", [rrow, ones_f], [pb], lambda: E["pe"].matmul(pb.t[:, :], ones_f.t[0:1, :], rrow.t[:], start=True, stop=True))
        P.op("act", [pb], [bc_sb], lambda: E["act"].activation(out=bc_sb.t[:], in_=pb.t[:], func=AF.Copy))
        for ch in range(4):
            P.op("dve", [kvp[ch], bc_sb, kvg], [kvnT],
                 lambda ch=ch: E["dve"].scalar_tensor_tensor(out=kvnT.t[:, ch, t0:t0 + 512], in0=kvp[ch].t[:], scalar=kvg.t[:, ch:ch + 1],
                                                             in1=bc_sb.t[:], op0=ALU.mult, op1=ALU.mult))
    P.pop()
    dump("kTa", kTa, kTa.t[:], [128, TW], BF16)
    dump("ikT", ikT, ikT.t[:], [128, TW], BF16)
    dump("krT", krT, krT.t[:], [128, TW], BF16)
    dump("kvnT", kvnT, kvnT.t[:], [128, 4, TW], BF16)
    dump("Va", Va, Va.t[:], [128, 16, 128], BF16)

    if stop_after == "B1":
        P.finish(list(outs.values()))
        return nc, es, used, outs

    P.push()
    x32r = Ring([P.sb("x32b_%d" % i, [128, 512], F32) for i in range(6)])
    uTq = P.sb("uTq", [128, NCH, TQ], BF16)
    P.dma("sp", uTq.t[:], u_d.t.ap().rearrange("c p t -> p c t"), [u_d], [uTq], uTq)
    qlT = P.sb("qlT", [128, 8, TQ], BF16)
    wqr = Ring([P.sb("wq%d" % i, [128, NCH, 256], BF16) for i in range(2)])
    stg = Ring([P.sb("stg%d" % i, [128, 512], BF16) for i in range(4)])
    QT = [(0, 512), (512, 512), (1024, 128)]
    w_in_v = IN.w_in.t.ap().rearrange("(kc p) n -> p kc n", p=128)

    def gemm_q(col0, ncols, evac):
        for g in range(ncols // 256):
            wb = wqr.next()
            P.dma("pool", wb.t[:], w_in_v[:, :, col0 + g * 256:col0 + (g + 1) * 256], [IN.w_in], [wb], wb)
            for cb in range(2):
                for (t0, n) in QT:
                    pt = psr.next()
                    for kc in range(NCH):
                        P.op("pe", [wb, uTq], [pt],
                             lambda kc=kc, pt=pt, wb=wb, cb=cb, t0=t0, n=n: E["pe"].matmul(
                                 pt.t[:, :n], wb.t[:, kc, cb * 128:(cb + 1) * 128], uTq.t[:, kc, t0:t0 + n],
                                 start=(kc == 0), stop=(kc == NCH - 1)), sig=(kc == NCH - 1))
                    evac(g * 2 + cb, pt, t0, n)

    def ev_aq(h, pt, t0, n):
        st = stg.next()
        P.op("act", [pt], [st], lambda: E["act"].activation(out=st.t[:, :n], in_=pt.t[:, :n], func=AF.Copy))
        P.dma("sp", qTa_d.t.ap()[h, :, t0:t0 + n], st.t[:, :n], [st], [qTa_d], st)

    def ev_iq(h, pt, t0, n):
        st = stg.next()
        rope_evac(pt, n, TQ0 + t0, st, st.t[:, :n])
        P.dma("sp", iqT_d.t.ap()[h, :, t0:t0 + n], st.t[:, :n], [st], [iqT_d], st)

    def ev_ql(ch, pt, t0, n):
        P.op("act", [pt], [qlT], lambda: E["act"].activation(out=qlT.t[:, ch, t0:t0 + n], in_=pt.t[:, :n], func=AF.Copy))

    gemm_q(A_Q, 2048, ev_aq)
    if stop_after == "B2a":
        P.barrier(); P.finish(list(outs.values())); return nc, es, used, outs
    gemm_q(I_Q, 4096, ev_iq)
    gemm_q(B_QL, 1024, ev_ql)
    P.push()
    wiw = P.sb("wiw", [128, NCH, 32], BF16)
    P.dma("pool", wiw.t[:], w_in_v[:, :, I_W:I_W + 32], [IN.w_in], [wiw], wiw)
    iw_d = P.dram("iw_d", [TQ, 32], F32)
    iwt = P.sb("iwt", [128, 32], F32)
    for j in range(NQB):
        pt = psr.next()
        for kc in range(NCH):
            P.op("pe", [wiw, uTq], [pt], lambda kc=kc, pt=pt, j=j: E["pe"].matmul(
                pt.t[:, 0:32], uTq.t[:, kc, j * 128:(j + 1) * 128], wiw.t[:, kc, :], start=(kc == 0), stop=(kc == NCH - 1)),
                sig=(kc == NCH - 1))
        P.op("act", [pt], [iwt], lambda pt=pt: E["act"].activation(out=iwt.t[:], in_=pt.t[:, 0:32], func=AF.Copy, scale=IW_SCALE))
        P.dma("sp", iw_d.t.ap()[j * 128:(j + 1) * 128, :], iwt.t[:], [iwt], [iw_d], iwt)
    if stop_after == "B2b":
        P.barrier(); P.finish(list(outs.values())); return nc, es, used, outs
    iwg = P.sb("iwg", [32, NQB, 4, 32], F32)
    P.dma("sp", iwg.t[:], iw_d.t.ap().rearrange("(j i g) h -> g j i h", j=NQB, i=4, g=32), [iw_d], [iwg], iwg)
    iwp = P.sb("iwp", [32, NQB, 128], F32)
    P.op("dve", [iwg], [iwp], lambda: E["dve"].tensor_copy(out=iwp.t[:].rearrange("g j (h i) -> g j i h", i=4), in_=iwg.t[:]))
    if stop_after == "B2b0":
        P.barrier(); P.finish(list(outs.values())); return nc, es, used, outs
    pt = psr.next()
    for j in range(NQB):
        P.op("pe", [iwp, ident], [pt], lambda j=j, pt=pt: E["pe"].transpose(pt.t[:, j * 32:(j + 1) * 32], iwp.t[:, j, :], ident.t[0:32, 0:32]))
    P.op("dve", [pt], [wT2], lambda pt=pt: E["dve"].tensor_copy(out=wT2.t[:], in_=pt.t[:, 0:288]))
    P.pop()
    if stop_after == "B2b1":
        P.barrier(); P.finish(list(outs.values())); return nc, es, used, outs
    qng = P.sb("qng", [128, 8], F32)
    P.dma("sp", qng.t[:], IN.qng_t.t.ap(), [IN.qng_t], [qng], qng)
    rrow = P.sb("rrowq", [1, 512], F32)
    rq_d = P.dram("rq_d", [NQB, 128], F32)
    sq = P.sb("sqq", [128, 512], F32)
    pss = psx[1]
    prq = psx[0]
    for (t0, n) in QT:
        for ch in range(8):
            P.op("act", [qlT], [sq], lambda ch=ch, t0=t0, n=n: E["act"].activation(out=sq.t[:, :n], in_=qlT.t[:, ch, t0:t0 + n], func=AF.Square))
            P.op("pe", [sq, ones_f], [pss], lambda ch=ch, n=n: E["pe"].matmul(pss.t[0:1, :n], ones_f.t[:, 0:1], sq.t[:, :n], start=(ch == 0), stop=(ch == 7)))
        P.op("act", [pss], [rrow], lambda n=n: E["act"].activation(out=rrow.t[:, :n], in_=pss.t[0:1, :n], func=AF.Sqrt, scale=1.0 / 1024.0, bias=1e-6))
        P.op("dve", [rrow], [rrow], lambda n=n: E["dve"].reciprocal(out=rrow.t[:, :n], in_=rrow.t[:, :n]))
        for bb in range(n // 128):
            j = t0 // 128 + bb
            P.dma("sp", rq_d.t.ap()[j:j + 1, :], rrow.t[0:1, bb * 128:(bb + 1) * 128], [rrow], [rq_d], rrow)
    if stop_after == "B2c1":
        P.barrier(); P.finish(list(outs.values())); return nc, es, used, outs
    rqg = P.sb("rqg", [NQB, 128], F32)
    P.dma("sp", rqg.t[:], rq_d.t.ap(), [rq_d], [rqg], rqg)
    P.op("pe", [rqg, ident], [prq], lambda: E["pe"].transpose(prq.t[:, 0:NQB], rqg.t[:], ident.t[0:NQB, 0:NQB]))
    P.op("dve", [prq], [rq], lambda: E["dve"].tensor_scalar(out=rq.t[:], in0=prq.t[:, 0:NQB], scalar1=MLA_SCALE, scalar2=None, op0=ALU.mult))
    if stop_after == "B2c2":
        P.barrier(); P.finish(list(outs.values())); return nc, es, used, outs
    for ch in range(8):
        P.op("dve", [qlT, qng], [qlT], lambda ch=ch: E["dve"].tensor_scalar(out=qlT.t[:, ch, :], in0=qlT.t[:, ch, :], scalar1=qng.t[:, ch:ch + 1], scalar2=None, op0=ALU.mult))
    if stop_after == "B2c":
        P.barrier(); P.finish(list(outs.values())); return nc, es, used, outs
    wuq_v = IN.w_uq.t.ap().rearrange("(kc p) n -> p kc n", p=128)
    wur = Ring([P.sb("wuq%d" % i, [128, 8, 192], BF16) for i in range(2)])
    for h in range(16):
        wb = wur.next()
        P.dma("pool", wb.t[:], wuq_v[:, :, h * 192:(h + 1) * 192], [IN.w_uq], [wb], wb)
        for part in range(2):
            for (t0, n) in QT:
                pt = psr.next()
                for kc in range(8):
                    P.op("pe", [wb, qlT], [pt], lambda kc=kc, pt=pt, wb=wb, part=part, t0=t0, n=n: E["pe"].matmul(
                        pt.t[:, :n], wb.t[:, kc, part * 64:part * 64 + 128], qlT.t[:, kc, t0:t0 + n], start=(kc == 0), stop=(kc == 7)), sig=(kc == 7))
                st = stg.next()
                if part == 0:
                    P.op("act", [pt], [st], lambda pt=pt, st=st, n=n: E["act"].activation(out=st.t[:, :n], in_=pt.t[:, :n], func=AF.Copy))
                    P.dma("sp", qTbn_d.t.ap()[h, :, t0:t0 + n], st.t[:, :n], [st], [qTbn_d], st)
                else:
                    rope_evac(pt, n, TQ0 + t0, st, st.t[64:128, :n], rows=slice(64, 128))
                    P.dma("sp", qTbr_d.t.ap()[h, 64:128, t0:t0 + n], st.t[64:128, :n], [st], [qTbr_d], st)
    P.pop()
    P.pop()
    ps_ab.close()
    dump("wT2", wT2, wT2.t[:], [128, 288])
    dump("rq", rq, rq.t[:], [128, NQB])
    if stop_after == "B2":
        for nm, bd, shp in (("qTa_d", qTa_d, [16, 128, TQ]), ("iqT_d", iqT_d, [32, 128, TQ]), ("qTbn_d", qTbn_d, [16, 128, TQ]), ("qTbr_d", qTbr_d, [16, 128, TQ])):
            if nm in dbg:
                o = eout("dbg_" + nm, shp, BF16)
                tmp = P.sb("tmp_" + nm, [128, shp[0], TQ], BF16)
                P.dma("sp", tmp.t[:], bd.t.ap().rearrange("h p t -> p h t"), [bd], [tmp], tmp)
                P.dma("sp", o.t.ap().rearrange("h p t -> p h t"), tmp.t[:], [tmp], [o], tmp)
                P.barrier()
        P.finish(list(outs.values()))
        return nc, es, used, outs
    P.push()
    pbig = P.ps("pbig", (128, 2048))
    psc = Ring([P.ps("psc%d" % i) for i in range(4)])
    maskK = P.sb("maskK", [128, TW], F32)
    P.dma("sp", maskK.t[:], IN.kvalid.t.ap().partition_broadcast(128), [IN.kvalid], [maskK], maskK)
    tri = P.sb("tri", [128, 128], F32)
    P.dma("sp", tri.t[:], IN.c_tri.t.ap(), [IN.c_tri], [tri], tri)
    wmask = P.sb("wmask", [128, 32, 128], BF16)
    P.dma("sp", wmask.t[:], IN.c_wmask.t.ap().rearrange("p (g t) -> p g t", g=32), [IN.c_wmask], [wmask], wmask)
    NBs = P.sb("NBs", [128, 16, 256], F32)
    dump("wmask", wmask, wmask.t[:], [128, 32, 128], BF16)
    P.push()
    relb = P.sb("relb", [32, 16], F32)
    relb31 = P.sb("relb31", [32, 16], F32)
    zb = P.sb("zb", [32, 384], F32)
    P.dma("sp", relb.t[:], IN.rel_bias.t.ap(), [IN.rel_bias], [relb], relb)
    P.dma("sp", relb31.t[:], IN.rel_bias.t.ap()[31:32, :].partition_broadcast(32), [IN.rel_bias], [relb31], relb31)
    P.dma("sp", zb.t[:], IN.c_zb.t.ap(), [IN.c_zb], [zb], zb)
    P.op("dve", [relb, relb31], [relb], lambda: E["dve"].tensor_tensor(out=relb.t[:], in0=relb.t[:], in1=relb31.t[:], op=ALU.subtract))
    P.op("dve", [relb], [relb], lambda: E["dve"].tensor_scalar(out=relb.t[:], in0=relb.t[:], scalar1=1.0 / ATT_SCALE, scalar2=None, op0=ALU.mult))
    for half in range(2):
        for sl in range(128):
            sp_ = half * 128 + sl
            P.op("pe", [zb, relb], [pbig], lambda sl=sl, sp_=sp_: E["pe"].matmul(
                pbig.t[:, sl * 16:(sl + 1) * 16], zb.t[:, 255 - sp_:255 - sp_ + 128], relb.t[:, :], start=True, stop=True))
        P.op("act", [pbig], [NBs], lambda half=half: E["act"].activation(
            out=NBs.t[:, :, half * 128:(half + 1) * 128], in_=pbig.t[:, :].rearrange("p (s h) -> p h s", h=16), func=AF.Copy))
    P.pop()

    def sm_exp(kend, lg, mxb, scale_ap_or_f):
        Pb, nmx, rs = Pbr.next(), nmxr.next(), rsr.next()
        if isinstance(scale_ap_or_f, float):
            P.op("dve", [mxb], [nmx], lambda: E["dve"].tensor_scalar(out=nmx.t[:], in0=mxb.t[:], scalar1=-scale_ap_or_f, scalar2=None, op0=ALU.mult))
            P.op("act", [lg, nmx], [Pb, rs], lambda: E["act"].activation(out=Pb.t[:, :kend], in_=lg.t[:, :kend], func=AF.Exp,
                                                                         scale=scale_ap_or_f, bias=nmx.t[:, 0:1], accum_out=rs.t[:, 0:1]))
        else:
            sbuf, sap = scale_ap_or_f
            P.op("dve", [mxb, sbuf], [nmx], lambda: E["dve"].tensor_scalar(out=nmx.t[:], in0=mxb.t[:], scalar1=sap, scalar2=-1.0, op0=ALU.mult, op1=ALU.mult))
            P.op("act", [lg, nmx, sbuf], [Pb, rs], lambda: E["act"].activation(out=Pb.t[:, :kend], in_=lg.t[:, :kend], func=AF.Exp,
                                                                               scale=sap, bias=nmx.t[:, 0:1], accum_out=rs.t[:, 0:1]))
        return Pb, rs

    def pv_stage(kend, Pb, rs, V_of, ydst_buf, ydst_ap):
        nkb = kend // 128
        PT, rinv, yb = PTr.next(), rinvr.next(), ybr.next()
        for k4 in range((nkb + 3) // 4):
            nb4 = min(4, nkb - k4 * 4)
            ptp = psc.next()
            ptv = ptp.t[:].bitcast(BF16)
            for q4 in range(nb4):
                kb = k4 * 4 + q4
                P.op("pe", [Pb, identb], [ptp], lambda kb=kb, q4=q4, ptv=ptv: E["pe"].transpose(
                    ptv[:, q4 * 128:(q4 + 1) * 128], Pb.t[:, kb * 128:(kb + 1) * 128], identb.t[:]))
            P.op("dve", [ptp], [PT], lambda k4=k4, nb4=nb4, ptv=ptv: E["dve"].tensor_copy(
                out=PT.t[:, k4 * 4:k4 * 4 + nb4, :].rearrange("p k q -> p (k q)"), in_=ptv[:, 0:nb4 * 128]))
        po = psc.next()
        for kb in range(nkb):
            vbuf, vap = V_of(kb)
            P.op("pe", [PT, vbuf], [po], lambda kb=kb, vap=vap: E["pe"].matmul(po.t[:, 0:128], PT.t[:, kb, :], vap, start=(kb == 0), stop=(kb == nkb - 1)))
        P.op("dve", [rs], [rinv], lambda: E["dve"].reciprocal(out=rinv.t[:], in_=rs.t[:]))
        P.op("act", [po, rinv], [yb], lambda: E["act"].activation(out=yb.t[:], in_=po.t[:, 0:128], func=AF.Identity, scale=rinv.t[:, 0:1]))
        pty = psc.next()
        ptyv = pty.t[:].bitcast(BF16)
        P.op("pe", [yb, identb], [pty], lambda: E["pe"].transpose(ptyv[:, 0:128], yb.t[:], identb.t[:]))
        P.op("act", [pty], [ydst_buf], lambda: E["act"].activation(out=ydst_ap, in_=ptyv[:, 0:128], func=AF.Copy))

    def copy_max(kend):
        lg = lgr.next()
        mxb = mxr.next()
        P.op("dve", [pbig], [lg, mxb], lambda: E["dve"].tensor_scalar(
            out=lg.t[:, :kend], in0=pbig.t[:, :kend], scalar1=1.0, scalar2=None, op0=ALU.mult, op1=ALU.max, accum_out=mxb.t[:, 0:1]))
        return lg, mxb

    sc = P.sb("sc", [128, TW], F32)
    selbb = P.sb("selbb", [128, TW], BF16)
    lgr = Ring([P.sb("lg%d" % i, [128, TW], F32) for i in range(2)])
    Pbr = Ring([P.sb("Pb%d" % i, [128, TW], BF16) for i in range(3)])
    PTr = Ring([P.sb("PT%d" % i, [128, 16, 128], BF16) for i in range(2)])
    mxr = Ring([P.sb("mxb%d" % i, [128, 1], F32) for i in range(4)])
    nmxr = Ring([P.sb("nmx%d" % i, [128, 1], F32) for i in range(4)])
    rsr = Ring([P.sb("rs%d" % i, [128, 1], F32) for i in range(4)])
    rinvr = Ring([P.sb("rinv%d" % i, [128, 1], F32) for i in range(4)])
    ybr = Ring([P.sb("yb%d" % i, [128, 128], BF16) for i in range(2)])
    ystage = P.sb("ystage", [128, 16, 128], BF16)

    P.push()
    wk = [P.sb("wk%d" % i, [128, TW], F32) for i in range(2)]
    m8 = P.sb("m8", [128, 8], F32)
    thr = P.sb("thr", [128, 1], F32)
    iqr = Ring([P.sb("iqb%d" % i, [128, 32, 128], BF16) for i in range(2)])
    Wblk = P.sb("Wblk", [128, 32, 128], BF16)
    Rr = Ring([P.sb("Rr%d" % i, [128, 512], BF16) for i in range(4)])
    qar = Ring([P.sb("qab%d" % i, [128, 16, 128], BF16) for i in range(2)])
    wa2r = Ring([P.sb("wa2_%d" % i, [128, NCH, 128], BF16) for i in range(2)])
    nblocks = NQB if stop_after not in ("C1", "C2") else (2 if stop_after == "C1" else 0)
    maskKb = P.sb("maskKb", [128, TW], BF16)
    trib = P.sb("trib", [128, 128], BF16)
    P.op("dve", [maskK], [maskKb], lambda: E["dve"].tensor_copy(out=maskKb.t[:], in_=maskK.t[:]))
    P.op("dve", [tri], [trib], lambda: E["dve"].tensor_copy(out=trib.t[:], in_=tri.t[:]))
    scr = [sc, P.sb("sc_b", [128, TW], F32)]

    def indexer(j):
        kend = TQ0 + 128 * j + 128
        nkt = (kend + 511) // 512
        scb = scr[j % 2]
        iqb = iqr.next()
        P.dma("sp", iqb.t[:], iqT_d.t.ap().rearrange("h p t -> p h t")[:, :, j * 128:(j + 1) * 128], [iqT_d], [iqb], iqb)
        for g in range(32):
            P.op("act", [wmask, wT2], [Wblk], lambda g=g: E["act"].activation(
                out=Wblk.t[:, g, :], in_=wmask.t[:, g, :], func=AF.Identity, scale=wT2.t[:, 32 * j + g:32 * j + g + 1]))
        for kt in range(nkt):
            n = min(512, kend - kt * 512)
            Rs = {}

            def dots(g, n=n, kt=kt):
                pd = psc.next()
                P.op("pe", [iqb, ikT], [pd], lambda: E["pe"].matmul(
                    pd.t[:, :n], iqb.t[:].rearrange("p h t -> p (h t)")[:, g:4096:32], ikT.t[:, kt * 512:kt * 512 + n], start=True, stop=True))
                R = Rr.next()
                P.op("act", [pd], [R], lambda: E["act"].activation(out=R.t[:, :n], in_=pd.t[:, :n], func=AF.Relu))
                Rs[g] = R

            dots(0)
            dots(1)
            for g in range(32):
                if g + 2 < 32:
                    dots(g + 2)
                R = Rs.pop(g)
                P.op("pe", [Wblk, R], [pbig], lambda g=g, R=R, n=n, kt=kt: E["pe"].matmul(
                    pbig.t[:, kt * 512:kt * 512 + n], Wblk.t[:, g, :], R.t[:, :n], start=(g == 0), stop=False))
            P.op("pe", [identb, maskKb], [pbig], lambda n=n, kt=kt: E["pe"].matmul(
                pbig.t[:, kt * 512:kt * 512 + n], identb.t[:], maskKb.t[:, kt * 512:kt * 512 + n], start=False, stop=(kt < nkt - 1)))
        P.op("pe", [identb, trib], [pbig], lambda: E["pe"].matmul(pbig.t[:, kend - 128:kend], identb.t[:], trib.t[:], start=False, stop=True))
        for kt in range(nkt):
            n = min(512, kend - kt * 512)
            P.op("act", [pbig], [scb], lambda n=n, kt=kt: E["act"].activation(
                out=scb.t[:, kt * 512:kt * 512 + n], in_=pbig.t[:, kt * 512:kt * 512 + n], func=AF.Copy))

    if nblocks:
        indexer(0)
    for j in range(nblocks):
        T0 = TQ0 + 128 * j
        kend = T0 + 128
        nkt = (kend + 511) // 512
        sc = scr[j % 2]
        qab = qar.next()
        P.dma("sp", qab.t[:], qTa_d.t.ap().rearrange("h p t -> p h t")[:, :, j * 128:(j + 1) * 128], [qTa_d], [qab], qab)
        if j + 1 < nblocks:
            indexer(j + 1)
        if stop_after == "T_idx":
            continue
        emit_A2(4, wa2r, psc)
        src = sc
        for r in range(32):
            P.op("dve", [src], [m8], lambda src=src, kend=kend: E["dve"].max(out=m8.t[:], in_=src.t[:, :kend]))
            if r < 31:
                dst = wk[r % 2]
                P.op("dve", [src, m8], [dst], lambda src=src, dst=dst, kend=kend: E["dve"].match_replace(
                    out=dst.t[:, :kend], in_to_replace=m8.t[:], in_values=src.t[:, :kend], imm_value=-3.0e38))
                src = dst
        P.op("dve", [m8], [thr], lambda: E["dve"].tensor_scalar(out=thr.t[:], in0=m8.t[:, 7:8], scalar1=-1.0e29, scalar2=None, op0=ALU.max))
        P.op("dve", [sc, thr], [selbb], lambda kend=kend: E["dve"].tensor_scalar(
            out=selbb.t[:, :kend], in0=sc.t[:, :kend], scalar1=thr.t[:, 0:1], scalar2=MASKV, op0=ALU.is_lt, op1=ALU.mult))
        if stop_after == "T_topk":
            continue

        def dsa_A(h, kend=kend, nkt=nkt, qab=qab):
            for kt in range(nkt):
                n = min(512, kend - kt * 512)
                P.op("pe", [qab, kTa], [pbig], lambda kt=kt, n=n: E["pe"].matmul(
                    pbig.t[:, kt * 512:kt * 512 + n], qab.t[:, h, :], kTa.t[:, kt * 512:kt * 512 + n], start=True, stop=False))
            for kt in range(nkt):
                n = min(512, kend - kt * 512)
                P.op("pe", [identb, selbb], [pbig], lambda kt=kt, n=n: E["pe"].matmul(
                    pbig.t[:, kt * 512:kt * 512 + n], identb.t[:], selbb.t[:, kt * 512:kt * 512 + n], start=False, stop=False))
            for nb in range(2):
                c0 = kend - 256 + 128 * nb
                P.op("pe", [ident, NBs], [pbig], lambda nb=nb, c0=c0: E["pe"].matmul(
                    pbig.t[:, c0:c0 + 128], ident.t[:], NBs.t[:, h, nb * 128:(nb + 1) * 128], start=False, stop=True))
            lg, mxb = copy_max(kend)
            return sm_exp(kend, lg, mxb, ATT_SCALE)

        pend = [dsa_A(0), dsa_A(1)]
        for h in range(16):
            if h + 2 < 16:
                pend.append(dsa_A(h + 2))
            cur = pend.pop(0)
            pv_stage(kend, cur[0], cur[1], lambda kb: (Va, Va.t[:, kb, :]), ystage, ystage.t[:, h, :])
        P.dma("sp", yT_d.t.ap().rearrange("h p t -> p h t")[:, 0:16, j * 128:(j + 1) * 128], ystage.t[:], [ystage], [yT_d], ystage)
    P.pop()
    if stop_after in ("T_idx", "T_topk", "T_dsa"):
        P.barrier(); P.finish(list(outs.values())); return nc, es, used, outs
    P.push()
    maskM = P.sb("maskM", [128, TW], BF16)
    triM = P.sb("triM", [128, 128], BF16)
    P.op("dve", [maskK], [maskM], lambda: E["dve"].tensor_scalar(out=maskM.t[:], in0=maskK.t[:], scalar1=MASKV / NEG, scalar2=None, op0=ALU.mult))
    P.op("dve", [tri], [triM], lambda: E["dve"].tensor_scalar(out=triM.t[:], in0=tri.t[:], scalar1=MASKV / NEG, scalar2=None, op0=ALU.mult))
    wkvr = Ring([P.sb("wkv%d" % i, [128, 4, 256], BF16) for i in range(2)])
    kThr = Ring([P.sb("kTh%d" % i, [128, TW], BF16) for i in range(2)])
    Vhr = Ring([P.sb("Vh%d" % i, [128, 16, 128], BF16) for i in range(2)])
    qbnr = Ring([P.sb("qbn%d" % i, [128, TQ], BF16) for i in range(2)])
    qbrr = Ring([P.sb("qbr%d" % i, [128, TQ], BF16) for i in range(2)])
    for b_ in qbrr.bufs:
        P.op("dve", [], [b_], lambda b_=b_: E["dve"].memset(b_.t[0:64, :], 0.0))
    ymla = Ring([P.sb("ymla%d" % i, [128, TQ], BF16) for i in range(2)])
    wa2rM = Ring([P.sb("wa2m_%d" % i, [128, NCH, 128], BF16) for i in range(3)])
    wukv_v = IN.w_ukv.t.ap().rearrange("(kc p) n -> p kc n", p=128)
    nheads = 16 if stop_after != "C2" else 2
    hstate = {}

    def mla_setup(h):
        wkv = wkvr.next()
        P.dma("pool", wkv.t[:], wukv_v[:, :, h * 256:(h + 1) * 256], [IN.w_ukv], [wkv], wkv, ndesc=32)
        qbn, qbr, kTh, Vh, ym = qbnr.next(), qbrr.next(), kThr.next(), Vhr.next(), ymla.next()
        P.dma("sp", qbn.t[:], qTbn_d.t.ap()[h], [qTbn_d], [qbn], qbn)
        P.dma("sp", qbr.t[64:128, :], qTbr_d.t.ap()[h, 64:128, :], [qTbr_d], [qbr], qbr)
        for tt in range(4):
            pt = psc.next()
            for kc in range(4):
                P.op("pe", [wkv, kvnT], [pt], lambda kc=kc, tt=tt, pt=pt: E["pe"].matmul(
                    pt.t[:, :], wkv.t[:, kc, 0:128], kvnT.t[:, kc, tt * 512:(tt + 1) * 512], start=(kc == 0), stop=(kc == 3)))
            P.op("act", [pt], [kTh], lambda tt=tt, pt=pt: E["act"].activation(out=kTh.t[:, tt * 512:(tt + 1) * 512], in_=pt.t[:], func=AF.Copy))
        for b4 in range(4):
            pv = psc.next()
            for q4 in range(4):
                blk = b4 * 4 + q4
                for kc in range(4):
                    P.op("pe", [wkv, kvnT], [pv], lambda kc=kc, q4=q4, blk=blk, pv=pv: E["pe"].matmul(
                        pv.t[:, q4 * 128:(q4 + 1) * 128], kvnT.t[:, kc, blk * 128:(blk + 1) * 128], wkv.t[:, kc, 128:256], start=(kc == 0), stop=(kc == 3)))
            P.op("dve", [pv], [Vh], lambda b4=b4, pv=pv: E["dve"].tensor_copy(out=Vh.t[:, b4 * 4:(b4 + 1) * 4, :].rearrange("p k d -> p (k d)"), in_=pv.t[:, :]))
        hstate[h] = (qbn, qbr, kTh, Vh, ym)

    def mla_A(h, j):
        if j == 0:
            mla_setup(h)
        qbn, qbr, kTh, Vh, ym = hstate[h]
        kend = TQ0 + 128 * j + 128
        nkt = (kend + 511) // 512
        tiles = [(kt, min(512, kend - kt * 512)) for kt in range(nkt)]
        for kt, n in tiles:
            P.op("pe", [qbn, kTh], [pbig], lambda kt=kt, n=n: E["pe"].matmul(
                pbig.t[:, kt * 512:kt * 512 + n], qbn.t[:, j * 128:(j + 1) * 128], kTh.t[:, kt * 512:kt * 512 + n], start=True, stop=False))
        for kt, n in tiles:
            P.op("pe", [qbr, krT], [pbig], lambda kt=kt, n=n: E["pe"].matmul(
                pbig.t[:, kt * 512:kt * 512 + n], qbr.t[:, j * 128:(j + 1) * 128], krT.t[:, kt * 512:kt * 512 + n], start=False, stop=False))
        for kt, n in tiles:
            P.op("pe", [identb, maskM], [pbig], lambda kt=kt, n=n: E["pe"].matmul(
                pbig.t[:, kt * 512:kt * 512 + n], identb.t[:], maskM.t[:, kt * 512:kt * 512 + n], start=False, stop=False))
        P.op("pe", [identb, triM], [pbig], lambda: E["pe"].matmul(pbig.t[:, kend - 128:kend], identb.t[:], triM.t[:], start=False, stop=True))
        lg, mxb = copy_max(kend)
        Pb, rs = sm_exp(kend, lg, mxb, (rq, rq.t[:, j:j + 1]))
        return (kend, Pb, rs, Vh, ym)

    items = [(h, j) for h in range(nheads) for j in range(NQB)]
    pend = [mla_A(*it) for it in items[:2]]
    for i, (h, j) in enumerate(items):
        if i + 2 < len(items):
            pend.append(mla_A(*items[i + 2]))
        kend_, Pb_, rs_, Vh_, ym_ = pend.pop(0)
        pv_stage(kend_, Pb_, rs_, lambda kb, Vh_=Vh_: (Vh_, Vh_.t[:, kb, :]), ym_, ym_.t[:, j * 128:(j + 1) * 128])
        if j == NQB - 1:
            P.dma("sp", yT_d.t.ap()[16 + h], ym_.t[:], [ym_], [yT_d], ym_)
        emit_A2(1, wa2rM, psc)
    emit_A2(200, wa2rM, psc)
    for jj in (2, 4, 5):
        P.op("dve", [modT], [modT], lambda jj=jj: E["dve"].tensor_scalar(out=modT.t[:, jj * 32:(jj + 1) * 32], in0=modT.t[:, jj * 32:(jj + 1) * 32],
                                                                          scalar1=1.0, scalar2=None, op0=ALU.add))
    P.pop()
    P.pop()
    if stop_after == "C2":
        o = eout("dbg_yT", [32, 128, TQ], BF16)
        tmp = P.sb("tmp_y", [128, 32, TQ], BF16)
        P.dma("sp", tmp.t[:], yT_d.t.ap().rearrange("h p t -> p h t"), [yT_d], [tmp], tmp)
        P.dma("sp", o.t.ap().rearrange("h p t -> p h t"), tmp.t[:], [tmp], [o], tmp)
        P.barrier()
        P.finish(list(outs.values()))
        return nc, es, used, outs
    P.pop()
    ps_de = ExitStack()
    psr = Ring([Buf(ps_de.enter_context(nc.psum_tensor("psd%d" % i, [128, 512], F32))) for i in range(6)])
    psx = [Buf(ps_de.enter_context(nc.psum_tensor("psy%d" % i, [128, 512], F32))) for i in range(2)]
    rT_d = P.dram("rT_d", [NCH, 128, TQ], F32)
    x1T_d = P.dram("x1T_d", [NCH, 128, TQ], F32)
    r2T_d = P.dram("r2T_d", [NCH, 128, 1024], F32)

    def ln_pass(src_d, ntok, tile_n, g_in, b_in, consume, tile_begin=None, tile_end=None):
        P.push()
        gT = P.sb("ln_g", [128, NCH], F32)
        bT = P.sb("ln_b", [128, NCH], F32)
        P.dma("sp", gT.t[:], g_in.t.ap(), [g_in], [gT], gT)
        P.dma("sp", bT.t[:], b_in.t.ap(), [b_in], [bT], bT)
        rall = P.sb("ln_r", [128, NCH, tile_n], F32)
        sqr_ = Ring([P.sb("ln_sq%d" % i, [128, tile_n], F32) for i in range(2)])
        rows = {k: P.sb("ln_row_" + k, [1, tile_n], F32) for k in ("m", "msq", "v", "r", "nmr")}
        bc1 = P.sb("ln_bc1", [128, tile_n], F32)
        bc2 = P.sb("ln_bc2", [128, tile_n], F32)
        xo = Ring([P.sb("ln_xo%d" % i, [128, tile_n], F32) for i in range(2)])
        ps1, ps2 = psx[0], psx[1]
        for t0 in range(0, ntok, tile_n):
            n = min(tile_n, ntok - t0)
            if tile_begin is not None:
                tile_begin(t0, n)
            P.dma("sp", rall.t[:, :, :n], src_d.t.ap().rearrange("c p t -> p c t")[:, :, t0:t0 + n], [src_d], [rall], rall)
            for c in range(NCH):
                P.op("pe", [rall, ones_f], [ps1], lambda c=c, n=n: E["pe"].matmul(ps1.t[0:1, :n], ones_f.t[:, 0:1], rall.t[:, c, :n], start=(c == 0), stop=(c == NCH - 1)))
                sq = sqr_.next()
                P.op("act", [rall], [sq], lambda c=c, n=n, sq=sq: E["act"].activation(out=sq.t[:, :n], in_=rall.t[:, c, :n], func=AF.Square))
                P.op("pe", [sq, ones_f], [ps2], lambda c=c, n=n, sq=sq: E["pe"].matmul(ps2.t[0:1, :n], ones_f.t[:, 0:1], sq.t[:, :n], start=(c == 0), stop=(c == NCH - 1)))
            m, msq, v, rr, nmr = rows["m"], rows["msq"], rows["v"], rows["r"], rows["nmr"]
            P.op("act", [ps1], [m], lambda n=n: E["act"].activation(out=m.t[:, :n], in_=ps1.t[0:1, :n], func=AF.Copy, scale=1.0 / D))
            P.op("dve", [m], [msq], lambda n=n: E["dve"].tensor_tensor(out=msq.t[:, :n], in0=m.t[:, :n], in1=m.t[:, :n], op=ALU.mult))
            P.op("dve", [ps2, msq], [v], lambda n=n: E["dve"].scalar_tensor_tensor(out=v.t[:, :n], in0=ps2.t[0:1, :n], scalar=1.0 / D, in1=msq.t[:, :n], op0=ALU.mult, op1=ALU.subtract))
            P.op("act", [v], [v], lambda n=n: E["act"].activation(out=v.t[:, :n], in_=v.t[:, :n], func=AF.Sqrt, bias=1e-5))
            P.op("dve", [v], [rr], lambda n=n: E["dve"].reciprocal(out=rr.t[:, :n], in_=v.t[:, :n]))
            P.op("dve", [m, rr], [nmr], lambda n=n: E["dve"].scalar_tensor_tensor(out=nmr.t[:, :n], in0=m.t[:, :n], scalar=-1.0, in1=rr.t[:, :n], op0=ALU.mult, op1=ALU.mult))
            P.op("pe", [rr, ones_f], [ps1], lambda n=n: E["pe"].matmul(ps1.t[:, :n], ones_f.t[0:1, :], rr.t[:, :n], start=True, stop=True))
            P.op("pe", [nmr, ones_f], [ps2], lambda n=n: E["pe"].matmul(ps2.t[:, :n], ones_f.t[0:1, :], nmr.t[:, :n], start=True, stop=True))
            for c in range(NCH):
                P.op("dve", [rall, ps1], [rall], lambda c=c, n=n: E["dve"].tensor_tensor(out=rall.t[:, c, :n], in0=rall.t[:, c, :n], in1=ps1.t[:, :n], op=ALU.mult))
                P.op("dve", [rall, ps2], [rall], lambda c=c, n=n: E["dve"].tensor_tensor(out=rall.t[:, c, :n], in0=rall.t[:, c, :n], in1=ps2.t[:, :n], op=ALU.add))
                x = xo.next()
                P.op("act", [rall, gT, bT], [x], lambda c=c, n=n, x=x: E["act"].activation(
                    out=x.t[:, :n], in_=rall.t[:, c, :n], func=AF.Identity, scale=gT.t[:, c:c + 1], bias=bT.t[:, c:c + 1]))
                consume(c, t0, n, x)
            if tile_end is not None:
                tile_end(t0, n)
        P.pop()

    P.push()
    yT = P.sb("yT", [128, 32, TQ], BF16)
    P.dma("sp", yT.t[:], yT_d.t.ap().rearrange("h p t -> p h t"), [yT_d], [yT], yT)
    wor = Ring([P.sb("wo%d" % i, [128, NCH, 256], BF16) for i in range(2)])
    xtr = Ring([P.sb("xt%d" % i, [128, 512], F32) for i in range(3)])
    rtr = Ring([P.sb("rt%d" % i, [128, 512], F32) for i in range(3)])
    QT = [(0, 512), (512, 512), (1024, 128)]
    wo_v = IN.w_o.t.ap().rearrange("(kc p) n -> p kc n", p=128)
    for cg in range(16):
        wb = wor.next()
        P.dma("pool", wb.t[:], wo_v[:, :, cg * 256:(cg + 1) * 256], [IN.w_o], [wb], wb)
        for cb in range(2):
            dc = cg * 2 + cb
            for (t0, n) in QT:
                pt = psr.next()
                for kc in range(NCH):
                    P.op("pe", [wb, yT], [pt], lambda kc=kc, pt=pt, wb=wb, cb=cb, t0=t0, n=n: E["pe"].matmul(
                        pt.t[:, :n], wb.t[:, kc, cb * 128:(cb + 1) * 128], yT.t[:, kc, t0:t0 + n], start=(kc == 0), stop=(kc == NCH - 1)), sig=(kc == NCH - 1))
                xt = xtr.next()
                P.dma("sp", xt.t[:, :n], xT_d.t.ap()[dc, :, t0:t0 + n], [xT_d], [xt], xt)
                P.op("act", [xt], [xt], lambda xt=xt, n=n: E["act"].activation(out=xt.t[:, :n], in_=xt.t[:, :n], func=AF.Copy, scale=ALPHA))
                rt = rtr.next()
                P.op("dve", [pt, xt, modT], [rt], lambda pt=pt, xt=xt, rt=rt, n=n, dc=dc: E["dve"].scalar_tensor_tensor(
                    out=rt.t[:, :n], in0=pt.t[:, :n], scalar=modT.t[:, G_A + dc:G_A + dc + 1], in1=xt.t[:, :n], op0=ALU.mult, op1=ALU.add))
                P.dma("sp", rT_d.t.ap()[dc, :, t0:t0 + n], rt.t[:, :n], [rt], [rT_d], rt)
    P.pop()
    u2r = None

    def ln1_consume(c, t0, n, x):
        P.dma("sp", x1T_d.t.ap()[c, :, t0:t0 + n], x.t[:, :n], [x], [x1T_d], x)
        u2 = u2ring.next()
        P.op("dve", [x, modT], [u2], lambda: E["dve"].tensor_scalar(out=u2.t[:, :n], in0=x.t[:, :n], scalar1=modT.t[:, SC_F + c:SC_F + c + 1],
                                                                   scalar2=modT.t[:, SH_F + c:SH_F + c + 1], op0=ALU.mult, op1=ALU.add))
        P.dma("sp", u2T_d.t.ap()[c, :, t0:t0 + n], u2.t[:, :n], [u2], [u2T_d], u2)

    P.push()
    u2ring = Ring([P.sb("u2s%d" % i, [128, 384], BF16) for i in range(3)])
    ln_pass(rT_d, TQ, 384, IN.ln1g_t, IN.ln1b_t, ln1_consume)
    P.pop()
    if stop_after == "D":
        o = eout("dbg_x1T", [NCH, 128, TQ], F32)
        tmp = P.sb("tmp_x", [128, TQ], F32)
        for c in range(NCH):
            P.dma("sp", tmp.t[:], x1T_d.t.ap()[c], [x1T_d], [tmp], tmp)
            P.dma("sp", o.t.ap()[c], tmp.t[:], [tmp], [o], tmp)
        P.barrier()
        P.finish(list(outs.values()))
        return nc, es, used, outs
    P.push()
    u2 = P.sb("u2T", [128, NCH, 1026], BF16)
    P.dma("sp", u2.t[:], u2T_d.t.ap().rearrange("c p t -> p c t")[:, :, 126:TQ], [u2T_d], [u2], u2)
    cw = P.sb("cw", [128, 3 * NFF], F32)
    cbias = P.sb("cbias", [128, NFF], F32)
    hfl = P.sb("hfl", [128, 1], F32)
    P.dma("sp", cw.t[:], IN.convw_t.t.ap(), [IN.convw_t], [cw], cw)
    P.dma("sp", cbias.t[:], IN.convb_t.t.ap(), [IN.convb_t], [cbias], cbias)
    P.dma("sp", hfl.t[:], IN.halo.t.ap(), [IN.halo], [hfl], hfl)
    wgr = Ring([P.sb("wg%d" % i, [128, NCH, 256], BF16) for i in range(2)])
    wupr = Ring([P.sb("wu%d" % i, [128, NCH, 256], BF16) for i in range(2)])
    hgr = Ring([P.sb("hgp%d" % i, [128, 1026], F32) for i in range(2)])
    gar = Ring([P.sb("ga%d" % i, [128, 1024], F32) for i in range(2)])
    sgr = Ring([P.sb("sg%d" % i, [128, 1024], F32) for i in range(2)])
    hsr = Ring([P.sb("hst%d" % i, [128, 1024], BF16) for i in range(2)])
    wg_v = IN.w_gate.t.ap().rearrange("(kc p) n -> p kc n", p=128)
    wu_v = IN.w_up.t.ap().rearrange("(kc p) n -> p kc n", p=128)
    nfg = NFF // 2
    for fg in range(nfg):
        wg = wgr.next()
        wu = wupr.next()
        P.dma("pool", wg.t[:], wg_v[:, :, fg * 256:(fg + 1) * 256], [IN.w_gate], [wg], wg)
        P.dma("pool", wu.t[:], wu_v[:, :, fg * 256:(fg + 1) * 256], [IN.w_up], [wu], wu)
        for cb in range(2):
            f = fg * 2 + cb
            hgp = hgr.next()
            ph = psr.next()
            for kc in range(NCH):
                P.op("pe", [wg, u2], [ph], lambda kc=kc, ph=ph, wg=wg, cb=cb: E["pe"].matmul(
                    ph.t[:, 0:2], wg.t[:, kc, cb * 128:(cb + 1) * 128], u2.t[:, kc, 0:2], start=(kc == 0), stop=(kc == NCH - 1)), sig=(kc == NCH - 1))
            P.op("act", [ph, hfl], [hgp], lambda ph=ph, hgp=hgp: E["act"].activation(out=hgp.t[:, 0:2], in_=ph.t[:, 0:2], func=AF.Identity, scale=hfl.t[:, 0:1]))
            for tt in range(2):
                pg = psr.next()
                for kc in range(NCH):
                    P.op("pe", [wg, u2], [pg], lambda kc=kc, pg=pg, wg=wg, cb=cb, tt=tt: E["pe"].matmul(
                        pg.t[:, :], wg.t[:, kc, cb * 128:(cb + 1) * 128], u2.t[:, kc, 2 + tt * 512:2 + (tt + 1) * 512],
                        start=(kc == 0), stop=(kc == NCH - 1)), sig=(kc == NCH - 1))
                P.op("act", [pg], [hgp], lambda pg=pg, hgp=hgp, tt=tt: E["act"].activation(out=hgp.t[:, 2 + tt * 512:2 + (tt + 1) * 512], in_=pg.t[:], func=AF.Copy))
            ga = gar.next()
            P.op("act", [hgp, cw, cbias], [ga], lambda hgp=hgp, ga=ga, f=f: E["act"].activation(
                out=ga.t[:], in_=hgp.t[:, 2:1026], func=AF.Identity, scale=cw.t[:, 2 * NFF + f:2 * NFF + f + 1], bias=cbias.t[:, f:f + 1]))
            P.op("dve", [hgp, cw, ga], [ga], lambda hgp=hgp, ga=ga, f=f: E["dve"].scalar_tensor_tensor(
                out=ga.t[:], in0=hgp.t[:, 1:1025], scalar=cw.t[:, NFF + f:NFF + f + 1], in1=ga.t[:], op0=ALU.mult, op1=ALU.add))
            P.op("dve", [hgp, cw, ga], [ga], lambda hgp=hgp, ga=ga, f=f: E["dve"].scalar_tensor_tensor(
                out=ga.t[:], in0=hgp.t[:, 0:1024], scalar=cw.t[:, f:f + 1], in1=ga.t[:], op0=ALU.mult, op1=ALU.add))
            sg = sgr.next()
            P.op("act", [ga], [sg], lambda ga=ga, sg=sg: E["act"].activation(out=sg.t[:], in_=ga.t[:], func=AF.Silu))
            hst = hsr.next()
            for tt in range(2):
                pu = psr.next()
                for kc in range(NCH):
                    P.op("pe", [wu, u2], [pu], lambda kc=kc, pu=pu, wu=wu, cb=cb, tt=tt: E["pe"].matmul(
                        pu.t[:, :], wu.t[:, kc, cb * 128:(cb + 1) * 128], u2.t[:, kc, 2 + tt * 512:2 + (tt + 1) * 512],
                        start=(kc == 0), stop=(kc == NCH - 1)), sig=(kc == NCH - 1))
                P.op("dve", [sg, pu], [hst], lambda sg=sg, pu=pu, hst=hst, tt=tt: E["dve"].tensor_tensor(
                    out=hst.t[:, tt * 512:(tt + 1) * 512], in0=sg.t[:, tt * 512:(tt + 1) * 512], in1=pu.t[:], op=ALU.mult))
            P.dma("sp", hT_d.t.ap()[f], hst.t[:], [hst], [hT_d], hst)
    P.pop()
    P.push()
    hTt = P.sb("hTt", [128, NFF, 512], BF16)
    wdr = Ring([P.sb("wd%d" % i, [128, NFF, 256], BF16) for i in range(2)])
    xtr = Ring([P.sb("x1t%d" % i, [128, 512], F32) for i in range(3)])
    rtr = Ring([P.sb("r2t%d" % i, [128, 512], F32) for i in range(3)])
    wd_v = IN.w_down.t.ap().rearrange("(f p) n -> p f n", p=128)
    for tt in range(2):
        P.dma("sp", hTt.t[:], hT_d.t.ap().rearrange("f p t -> p f t")[:, :, tt * 512:(tt + 1) * 512], [hT_d], [hTt], hTt)
        for cg in range(16):
            wd = wdr.next()
            P.dma("pool", wd.t[:], wd_v[:, :, cg * 256:(cg + 1) * 256], [IN.w_down], [wd], wd, ndesc=688)
            for cb in range(2):
                dc = cg * 2 + cb
                pt = psr.next()
                for f in range(NFF):
                    P.op("pe", [wd, hTt], [pt], lambda f=f, pt=pt, wd=wd, cb=cb: E["pe"].matmul(
                        pt.t[:, :], wd.t[:, f, cb * 128:(cb + 1) * 128], hTt.t[:, f, :], start=(f == 0), stop=(f == NFF - 1)), sig=(f == NFF - 1))
                xt = xtr.next()
                P.dma("sp", xt.t[:], x1T_d.t.ap()[dc, :, 128 + tt * 512:128 + (tt + 1) * 512], [x1T_d], [xt], xt)
                P.op("act", [xt], [xt], lambda xt=xt: E["act"].activation(out=xt.t[:], in_=xt.t[:], func=AF.Copy, scale=ALPHA))
                rt = rtr.next()
                P.op("dve", [pt, xt, modT], [rt], lambda pt=pt, xt=xt, rt=rt, dc=dc: E["dve"].scalar_tensor_tensor(
                    out=rt.t[:], in0=pt.t[:], scalar=modT.t[:, G_F + dc:G_F + dc + 1], in1=xt.t[:], op0=ALU.mult, op1=ALU.add))
                P.dma("sp", r2T_d.t.ap()[dc, :, tt * 512:(tt + 1) * 512], rt.t[:], [rt], [r2T_d], rt)
    P.pop()
    if stop_after == "E1":
        o = eout("dbg_r2", [4, 128, 1024], F32)
        tmp = P.sb("tmp_r2", [128, 1024], F32)
        for c in range(4):
            P.dma("sp", tmp.t[:], r2T_d.t.ap()[c], [r2T_d], [tmp], tmp)
            P.dma("sp", o.t.ap()[c], tmp.t[:], [tmp], [o], tmp)
        o2 = eout("dbg_hT", [4, 128, 1024], BF16)
        tmp2 = P.sb("tmp_h", [128, 1024], BF16)
        for c in range(4):
            P.dma("sp", tmp2.t[:], hT_d.t.ap()[c], [hT_d], [tmp2], tmp2)
            P.dma("sp", o2.t.ap()[c], tmp2.t[:], [tmp2], [o2], tmp2)
        P.barrier()
        P.finish(list(outs.values()))
        return nc, es, used, outs
    P.push()
    orow = [P.sb("orow%d" % i, [128, D], F32) for i in range(2)]

    def ln2_consume(c, t0, n, x):
        for bb in range(n // 128):
            pt = psr.next()
            P.op("pe", [x, ident], [pt], lambda pt=pt, bb=bb: E["pe"].transpose(pt.t[:, 0:128], x.t[:, bb * 128:(bb + 1) * 128], ident.t[:]))
            ob = orow[bb]
            if (c + bb) % 2 == 0:
                P.op("act", [pt], [ob], lambda pt=pt, ob=ob: E["act"].activation(out=ob.t[:, c * 128:(c + 1) * 128], in_=pt.t[:, 0:128], func=AF.Copy))
            else:
                P.op("dve", [pt], [ob], lambda pt=pt, ob=ob: E["dve"].tensor_copy(out=ob.t[:, c * 128:(c + 1) * 128], in_=pt.t[:, 0:128]))

    def ln2_end(t0, n):
        for bb in range(n // 128):
            P.dma("sp", out_d.t.ap()[t0 + bb * 128:t0 + (bb + 1) * 128, :], orow[bb].t[:], [orow[bb]], [out_d], orow[bb])

    ln_pass(r2T_d, 1024, 256, IN.ln2g_t, IN.ln2b_t, ln2_consume, tile_end=ln2_end)
    P.pop()
    ps_de.close()
    P.barrier()
    P.finish(list(outs.values()))
    return nc, es, used, outs


def host_prep(inputs):
    f32 = np.float32
    x = np.asarray(inputs["x"], f32)
    c = np.asarray(inputs["c"], f32)
    pos = np.asarray(inputs["positions"], np.int32)

    def tl(v, nch):
        return np.ascontiguousarray(np.asarray(v, f32).reshape(nch, 128).T)

    shared = {
        "w_ada": np.asarray(inputs["w_ada"], f32)[0],
        "b_ada_t": tl(np.asarray(inputs["b_ada"])[0], 192),
        "w_in": np.asarray(inputs["w_in"], f32)[0],
        "rel_bias": np.asarray(inputs["rel_bias"], f32),
        "qng_t": tl(np.asarray(inputs["q_norm_g"])[0], 8),
        "w_uq": np.asarray(inputs["w_uq"], f32)[0],
        "kvg_t": tl(np.asarray(inputs["kv_norm_g"])[0], 4),
        "w_ukv": np.asarray(inputs["w_ukv"], f32)[0],
        "w_o": np.asarray(inputs["w_o"], f32)[0],
        "ln1g_t": tl(np.asarray(inputs["ln1_g"])[0], 32),
        "ln1b_t": tl(np.asarray(inputs["ln1_b"])[0], 32),
        "w_gate": np.asarray(inputs["w_gate"], f32)[0],
        "w_up": np.asarray(inputs["w_up"], f32)[0],
        "convw_t": np.ascontiguousarray(np.asarray(inputs["conv_w"], f32)[0].reshape(3, NFF, 128).transpose(2, 0, 1).reshape(128, 3 * NFF)),
        "convb_t": tl(np.asarray(inputs["conv_b"])[0], NFF),
        "w_down": np.asarray(inputs["w_down"], f32)[0],
        "ln2g_t": tl(np.asarray(inputs["ln2_g"])[0], 32),
        "ln2b_t": tl(np.asarray(inputs["ln2_b"])[0], 32),
    }
    shared["c_ident"] = np.eye(128, dtype=f32)
    rot = np.zeros((128, 128), f32)
    for m in range(64, 96):
        rot[m + 32, m] = -1.0
    for m in range(96, 128):
        rot[m - 32, m] = 1.0
    shared["c_rot"] = rot
    half = 32
    fr = (np.float32(10000.0) ** (-np.arange(half, dtype=f32) / np.float32(half))).astype(f32)
    shared["c_freq"] = np.ascontiguousarray(fr[np.arange(128) % 32].reshape(128, 1))
    wm = np.zeros((128, 32, 128), f32)
    for p in range(128):
        for g in range(32):
            wm[p, g, g + 32 * (p % 4)] = 1.0
    shared["c_wmask"] = wm.reshape(128, 32 * 128).astype(ml_dtypes.bfloat16)
    tri = np.where(np.arange(128)[None, :] <= np.arange(128)[:, None], 0.0, NEG).astype(f32)
    shared["c_tri"] = tri

    def bucket(n):
        if n < 16:
            return n
        v = 16 + int(np.float32(np.log(np.float32(n) / np.float32(16.0))) / np.float32(math.log(128 / 16)) * np.float32(16))
        return min(v, 31)

    zb = np.zeros((32, 384), f32)
    for xi in range(384):
        xv = min(max(xi - 127, 0), 128)
        zb[bucket(xv), xi] = 1.0
    shared["c_zb"] = zb

    per_core = []
    for core in range(8):
        b, hf = core // 2, core % 2
        start = hf * 1024
        xw = np.zeros((TW, D), f32)
        pw = np.zeros((1, TW), np.int32)
        kv = np.full((1, TW), NEG, f32)
        lo = start - 1024
        if lo < 0:
            xw[1024:] = x[b, 0:1024]
            pw[0, 1024:] = pos[b, 0:1024]
            kv[0, 1024:] = 0.0
        else:
            xw[:] = x[b, lo:lo + TW]
            pw[0, :] = pos[b, lo:lo + TW]
            kv[0, :] = 0.0
        m = dict(shared)
        m["xw"] = xw
        m["posw"] = pw
        m["kvalid"] = kv
        m["halo"] = np.full((128, 1), float(hf), f32)
        m["c_t"] = np.ascontiguousarray(c[b].reshape(NCH, 128).T)
        per_core.append(m)
    return per_core


def kernel(**inputs):
    nc, es, used, outs = build_program()
    per_core = [{k: m[k] for k in used} for m in host_prep(inputs)]
    with es:
        res = run_bass_kernel_spmd(nc, per_core, core_ids=list(range(8)))
    out = np.zeros((4, SEQ, D), np.float32)
    for core in range(8):
        b, hf = core // 2, core % 2
        out[b, hf * 1024:(hf + 1) * 1024] = res.results[core]["out"]
    return out
```
